# Optimizing a Trainium2 kernel written in Bass

```python
import jax, jax.numpy as jnp
from jax import lax
import numpy as np

D_MODEL = 1024
BATCH = 2
SEQ = 8192
DEPTH = 4
DEC_BATCH = 8
DEC_SEQ = 16
PAST_LEN = 4096

CHUNK = 64
N_A_LAYERS = DEPTH // 2
N_B_LAYERS = DEPTH - N_A_LAYERS
POOL_WINDOWS = (2, 4, 8, 16)
N_POOL_GROUPS = 4
POOL_GROUP_DIM = D_MODEL // N_POOL_GROUPS
POOL_HIST = max(POOL_WINDOWS) - 1
HEAD_DIM = 64
N_HEADS = D_MODEL // HEAD_DIM
N_KV_HEADS = 4
GQA_GROUP = N_HEADS // N_KV_HEADS
WINDOW = 128
WINDOW_CHUNKS = WINDOW // CHUNK
D_FF = -(-8 * D_MODEL // (3 * 256)) * 256
EPS = 1e-6
NEG_INF = -1e30

kernel_name = 'yoco_pool_swa_sink_stream_step'


def rms_norm(x, g):
    xf = x.astype(jnp.float32)
    y = xf * lax.rsqrt(jnp.mean(xf * xf, axis=-1, keepdims=True) + EPS)
    return (y * g.astype(jnp.float32)).astype(x.dtype)


def modulate(h, shift, scale):
    return h * (1 + scale[:, None, :]) + shift[:, None, :]


def pool_mixer(h, hist, pos0, w_pool, pool_scale):
    B, L, D = h.shape
    ext = jnp.concatenate([hist.astype(h.dtype), h], axis=1)
    csum = jnp.pad(jnp.cumsum(ext.astype(jnp.float32), axis=1), ((0, 0), (1, 0), (0, 0)))
    end = csum[:, POOL_HIST + 1:POOL_HIST + 1 + L, :]
    pos = pos0 + jnp.arange(L)
    parts = []
    for g, w in enumerate(POOL_WINDOWS):
        sl = slice(g * POOL_GROUP_DIM, (g + 1) * POOL_GROUP_DIM)
        start = csum[:, POOL_HIST + 1 - w:POOL_HIST + 1 - w + L, sl]
        count = jnp.minimum(w, pos + 1).astype(jnp.float32)[None, :, None]
        parts.append((end[..., sl] - start) / count)
    pooled = jnp.concatenate(parts, axis=-1)
    diff = (pooled - h.astype(jnp.float32)).astype(h.dtype).reshape(B, L, N_POOL_GROUPS, POOL_GROUP_DIM)
    out = jnp.einsum('blgc,gce->blge', diff, w_pool).reshape(B, L, D)
    return out * pool_scale


def window_attention(q, k_ext, v_ext, pos0, sinks):
    B, L = q.shape[0], q.shape[1]
    nb = -(-L // CHUNK)
    pad = nb * CHUNK - L
    qb = jnp.pad(q, ((0, 0), (0, pad), (0, 0), (0, 0))).reshape(B, nb, CHUNK, N_KV_HEADS, GQA_GROUP, HEAD_DIM)
    kblk = jnp.pad(k_ext, ((0, 0), (0, pad), (0, 0), (0, 0))).reshape(B, nb + WINDOW_CHUNKS, CHUNK, N_KV_HEADS, HEAD_DIM)
    vblk = jnp.pad(v_ext, ((0, 0), (0, pad), (0, 0), (0, 0))).reshape(B, nb + WINDOW_CHUNKS, CHUNK, N_KV_HEADS, HEAD_DIM)
    kb = jnp.concatenate([kblk[:, j:j + nb] for j in range(WINDOW_CHUNKS + 1)], axis=2)
    vb = jnp.concatenate([vblk[:, j:j + nb] for j in range(WINDOW_CHUNKS + 1)], axis=2)
    idx = jnp.arange((nb + WINDOW_CHUNKS) * CHUNK)
    valid = ((pos0 - WINDOW + idx) >= 0) & (idx < WINDOW + L)
    vmask = valid.reshape(nb + WINDOW_CHUNKS, CHUNK)
    mask = jnp.concatenate([vmask[j:j + nb] for j in range(WINDOW_CHUNKS + 1)], axis=1)
    logits = jnp.einsum('bnqkgd,bnskd->bnkgqs', qb.astype(jnp.float32), kb.astype(jnp.float32)) * (HEAD_DIM ** -0.5)
    logits = jnp.where(mask[None, :, None, None, None, :], logits, NEG_INF)
    sink = sinks.astype(jnp.float32).reshape(1, 1, N_KV_HEADS, GQA_GROUP, 1, 1)
    m = jnp.maximum(jnp.max(logits, axis=-1, keepdims=True), sink)
    p = jnp.exp(logits - m)
    denom = jnp.sum(p, axis=-1, keepdims=True) + jnp.exp(sink - m)
    out = jnp.einsum('bnkgqs,bnskd->bnqkgd', p / denom, vb.astype(jnp.float32))
    return out.reshape(B, nb * CHUNK, N_HEADS * HEAD_DIM)[:, :L].astype(q.dtype)


def swiglu(h, w_in, w_out):
    gate, up = jnp.split(h @ w_in, 2, axis=-1)
    return (jax.nn.silu(gate) * up) @ w_out


def trunk(x, c, pool_hist, k_hist, v_hist, pos0, w_ada, b_ada, g_mix, g_ffn, w_pool, pool_scale,
          w_q, g_q, sinks, w_o, g_kv, w_ada_kv, b_ada_kv, w_kv, g_k, w_ffn_in, w_ffn_out):
    B, L = x.shape[0], x.shape[1]
    c_act = jax.nn.silu(c)
    new_pool = []
    k_ext = None
    v_ext = None
    for i in range(DEPTH):
        ada = c_act @ w_ada[i] + b_ada[i]
        sh1, sc1, g1, sh2, sc2, g2 = jnp.split(ada, 6, axis=-1)
        h = modulate(rms_norm(x, g_mix[i]), sh1, sc1)
        if i < N_A_LAYERS:
            hist = pool_hist[i].astype(h.dtype)
            new_pool.append(jnp.concatenate([hist, h], axis=1)[:, -POOL_HIST:])
            mix = pool_mixer(h, hist, pos0, w_pool[i], pool_scale[i])
        else:
            j = i - N_A_LAYERS
            q = rms_norm((h @ w_q[j]).reshape(B, L, N_HEADS, HEAD_DIM), g_q[j])
            mix = window_attention(q, k_ext, v_ext, pos0, sinks[j]) @ w_o[j]
        x = x + g1[:, None, :] * mix
        h = modulate(rms_norm(x, g_ffn[i]), sh2, sc2)
        x = x + g2[:, None, :] * swiglu(h, w_ffn_in[i], w_ffn_out[i])
        if i == N_A_LAYERS - 1:
            sh_kv, sc_kv = jnp.split(c_act @ w_ada_kv + b_ada_kv, 2, axis=-1)
            hkv = modulate(rms_norm(x, g_kv), sh_kv, sc_kv)
            kv = (hkv @ w_kv).reshape(B, L, 2, N_KV_HEADS, HEAD_DIM)
            k_new = rms_norm(kv[:, :, 0], g_k)
            v_new = kv[:, :, 1]
            k_ext = jnp.concatenate([k_hist.astype(k_new.dtype), k_new], axis=1)
            v_ext = jnp.concatenate([v_hist.astype(v_new.dtype), v_new], axis=1)
    return x, jnp.stack(new_pool, axis=0), k_ext[:, -WINDOW:], v_ext[:, -WINDOW:]


def setup_inputs(seed: int = 0) -> dict:
    key = jax.random.key(seed)
    ks = jax.random.split(key, 32)
    nrm = jax.random.normal
    D = D_MODEL
    return {
        'x_prompt': nrm(ks[0], (BATCH, SEQ, D), jnp.float32),
        'x_sample': nrm(ks[1], (DEC_BATCH, DEC_SEQ, D), jnp.float32),
        'c_prompt': nrm(ks[2], (BATCH, D), jnp.float32),
        'c_sample': nrm(ks[3], (DEC_BATCH, D), jnp.float32),
        'state_pool': nrm(ks[4], (N_A_LAYERS, DEC_BATCH, POOL_HIST, D), jnp.float32),
        'cache_k': nrm(ks[5], (DEC_BATCH, WINDOW, N_KV_HEADS, HEAD_DIM), jnp.float32),
        'cache_v': nrm(ks[6], (DEC_BATCH, WINDOW, N_KV_HEADS, HEAD_DIM), jnp.float32),
        'w_ada': nrm(ks[7], (DEPTH, D, 6 * D), jnp.float32) * (0.3 * D ** -0.5),
        'b_ada': nrm(ks[8], (DEPTH, 6 * D), jnp.float32) * 0.02,
        'g_mix': 1.0 + 0.05 * nrm(ks[9], (DEPTH, D), jnp.float32),
        'g_ffn': 1.0 + 0.05 * nrm(ks[10], (DEPTH, D), jnp.float32),
        'w_pool': nrm(ks[11], (N_A_LAYERS, N_POOL_GROUPS, POOL_GROUP_DIM, POOL_GROUP_DIM), jnp.float32) * POOL_GROUP_DIM ** -0.5,
        'pool_scale': 1.0 + 0.1 * nrm(ks[12], (N_A_LAYERS, D), jnp.float32),
        'w_q': nrm(ks[13], (N_B_LAYERS, D, N_HEADS * HEAD_DIM), jnp.float32) * D ** -0.5,
        'g_q': 1.0 + 0.05 * nrm(ks[14], (N_B_LAYERS, HEAD_DIM), jnp.float32),
        'sinks': 0.5 * nrm(ks[15], (N_B_LAYERS, N_HEADS), jnp.float32),
        'w_o': nrm(ks[16], (N_B_LAYERS, N_HEADS * HEAD_DIM, D), jnp.float32) * (N_HEADS * HEAD_DIM) ** -0.5,
        'g_kv': 1.0 + 0.05 * nrm(ks[17], (D,), jnp.float32),
        'w_ada_kv': nrm(ks[18], (D, 2 * D), jnp.float32) * (0.3 * D ** -0.5),
        'b_ada_kv': nrm(ks[19], (2 * D,), jnp.float32) * 0.02,
        'w_kv': nrm(ks[20], (D, 2 * N_KV_HEADS * HEAD_DIM), jnp.float32) * D ** -0.5,
        'g_k': 1.0 + 0.05 * nrm(ks[21], (HEAD_DIM,), jnp.float32),
        'w_ffn_in': nrm(ks[22], (DEPTH, D, 2 * D_FF), jnp.float32) * D ** -0.5,
        'w_ffn_out': nrm(ks[23], (DEPTH, D_FF, D), jnp.float32) * D_FF ** -0.5,
    }


def reference(x_prompt, x_sample, c_prompt, c_sample, state_pool, cache_k, cache_v,
              w_ada, b_ada, g_mix, g_ffn, w_pool, pool_scale, w_q, g_q, sinks, w_o,
              g_kv, w_ada_kv, b_ada_kv, w_kv, g_k, w_ffn_in, w_ffn_out):
    B = x_prompt.shape[0]
    pool_zero = jnp.zeros((N_A_LAYERS, B, POOL_HIST, D_MODEL), x_prompt.dtype)
    kv_zero = jnp.zeros((B, WINDOW, N_KV_HEADS, HEAD_DIM), x_prompt.dtype)
    y_prompt, pool_p, k_p, v_p = trunk(x_prompt, c_prompt, pool_zero, kv_zero, kv_zero, 0,
                                       w_ada, b_ada, g_mix, g_ffn, w_pool, pool_scale, w_q, g_q, sinks, w_o,
                                       g_kv, w_ada_kv, b_ada_kv, w_kv, g_k, w_ffn_in, w_ffn_out)
    y_sample, pool_s, k_s, v_s = trunk(x_sample, c_sample, state_pool, cache_k, cache_v, PAST_LEN,
                                       w_ada, b_ada, g_mix, g_ffn, w_pool, pool_scale, w_q, g_q, sinks, w_o,
                                       g_kv, w_ada_kv, b_ada_kv, w_kv, g_k, w_ffn_in, w_ffn_out)
    return (y_prompt, y_sample, pool_p, k_p, v_p, pool_s, k_s, v_s)
```

```python
import numpy as np
from contextlib import ExitStack
import concourse.bass as bass
import concourse.mybir as mybir
from concourse.bass_utils import run_bass_kernel_spmd

F32 = mybir.dt.float32
BF16 = mybir.dt.bfloat16
AF = mybir.ActivationFunctionType
ALU = mybir.AluOpType

NCORES = 8
D = 1024
KC = 8
T = 2224
C_OWN = 176
NOWN = 2048
SEGB = [0, 16, 176, 688, 1200, 1712, 2224]
BLK_A = [(0, 176), (176, 688), (688, 1200), (1200, 1712), (1712, 2224)]
BLK_B = [(0, 16), (176, 688), (688, 1200), (1200, 1712), (1712, 2224)]
NJ = 22
GROUPS = [(0, 4), (4, 8), (8, 12), (12, 16), (16, 20), (20, 22)]
NR = 4
PTL = 2256
EPS = 1e-6
ROT = 2000
NV_TILES = 19

V_BADA = 0
V_GMIX = 192
V_GFFN = 224
V_PSC = 256
V_GKV = 272
V_BKV = 280
V_GQ = 296
V_GK = 298
V_SINK = 299
V_FLAG = 331
V_INVC = 332
V_EPS = 460
NVEC = 461


def segkeys(name, c0, c1, *extra):
    ks = []
    for i in range(len(SEGB) - 1):
        if c0 < SEGB[i + 1] and c1 > SEGB[i]:
            ks.append((name,) + tuple(extra) + (i,))
    return ks


def colsegs(c0, c1):
    out = []
    if c0 < 16:
        out.append((c0, min(c1, 16), 1))
    if c1 > 16:
        out.append((max(c0, 16), c1, 0))
    return out


class Sched:
    def __init__(self):
        self.ops = []

    enabled = True

    def add(self, eng, fn, r=(), w=(), dma=None):
        if not self.enabled:
            return -1
        self.ops.append((eng, fn, tuple(r), tuple(w), dma))
        return len(self.ops) - 1

    def analyze(self):
        ops = self.ops
        last_w = {}
        readers = {}
        need_all = []
        signal = [False] * len(ops)
        for k, (eng, fn, r, w, dma) in enumerate(ops):
            raw = set()
            oth = set()
            for key in r:
                if key in last_w:
                    raw.add(last_w[key])
                if key[0] == "ps":
                    for idx in readers.get(key, {}).values():
                        oth.add(idx)
            for key in w:
                if key in last_w:
                    oth.add(last_w[key])
                for idx in readers.get(key, {}).values():
                    oth.add(idx)
            for key in r:
                readers.setdefault(key, {})[eng if dma is None else ("dma", k)] = k
            for key in w:
                last_w[key] = k
                readers[key] = {}
            need_dma = set()
            need_eng = {}
            for d in raw | oth:
                if d == k:
                    continue
                de, ddma = ops[d][0], ops[d][4]
                if ddma is not None:
                    need_dma.add(d)
                elif de != eng or dma is not None:
                    need_eng[de] = max(need_eng.get(de, -1), d)
                elif eng in ("act", "dve", "pool"):
                    need_eng[de] = max(need_eng.get(de, -1), d)
            need = sorted(need_dma | set(need_eng.values()))
            for d in need:
                signal[d] = True
            need_all.append(need)
        cnt = {}
        dcnt = {}
        event = {}
        for k, (eng, fn, r, w, dma) in enumerate(ops):
            if dma is not None:
                dcnt[dma] = dcnt.get(dma, 0) + 16
                event[k] = (("dma", dma), dcnt[dma])
            elif signal[k]:
                c = cnt.get(eng, 0)
                cnt[eng] = c + 1
                event[k] = ((eng, c // ROT), c % ROT + 1)
        self.need_all = need_all
        self.event = event
        self.final_dma = dict(dcnt)
        semkeys = set(sk for sk, _ in event.values())
        return sorted(semkeys, key=str)


def build_program():
    nc = bass.Bass("TRN2", target_bir_lowering=False)
    dt = nc.dram_tensor
    xT_d = dt("xT", [D, T], F32, kind="ExternalInput").ap()
    cT_d = dt("cT", [128, 16], F32, kind="ExternalInput").ap()
    vecs_d = dt("vecs", [128, NVEC], F32, kind="ExternalInput").ap()
    spT_d = dt("spT", [128, 256], F32, kind="ExternalInput").ap()
    ckT_d = dt("ckT", [128, 512], F32, kind="ExternalInput").ap()
    cv_d = dt("cv", [128, 384], F32, kind="ExternalInput").ap()
    cvraw_d = dt("cvraw", [128, 256], F32, kind="ExternalInput").ap()
    ws_d = dt("wstream", [N_SLOTS, 128, 2048], F32, kind="ExternalInput").ap()
    yT_d = dt("yT", [D, 16 + NOWN], F32, kind="ExternalOutput").ap()
    poolT_d = dt("poolT", [2, D, 32], F32, kind="ExternalOutput").ap()
    knT_d = dt("knT", [4, 64, 144], F32, kind="ExternalOutput").ap()
    vn_d = dt("vn", [144, 256], F32, kind="ExternalOutput").ap()
    ksT_d = dt("ksT", [4, 64, 128], F32, kind="ExternalOutput").ap()
    vs_d = dt("vs", [128, 256], F32, kind="ExternalOutput").ap()

    S = Sched()
    es = ExitStack()
    with es:
        off = [0]
        NW = 53200
        arena = es.enter_context(nc.sbuf_tensor("arena", [128, NW], F32))

        def carve(nbytes):
            n = (nbytes + 3) // 4
            a = arena[:, off[0]:off[0] + n]
            off[0] += n
            return a

        xT = carve(KC * T * 4).rearrange("p (c t) -> p c t", c=KC)
        rstd = carve(T * 4)
        ring = carve(NR * 2048 * 2).bitcast(BF16).rearrange("p (s n) -> p s n", s=NR)
        r1raw = carve(KC * T * 2)
        R1 = r1raw.bitcast(BF16).rearrange("p (c t) -> p c t", c=KC)
        PA = r1raw[:, 0:PTL]
        PB = r1raw[:, PTL:2 * PTL]
        PC = r1raw[:, 2 * PTL:3 * PTL]
        X4 = carve(4 * T * 2).bitcast(BF16).rearrange("p (c t) -> p c t", c=4)
        kv_off = off[0]
        kT = carve(4 * T * 2).bitcast(BF16).rearrange("p (c t) -> p c t", c=4)
        Vt = carve(NV_TILES * 384 * 2).bitcast(BF16).rearrange("p (j n) -> p j n", j=NV_TILES)
        assert off[0] - kv_off >= 3 * PTL
        PSETS = [(PA, PB, PC), tuple(arena[:, kv_off + k * PTL:kv_off + (k + 1) * PTL] for k in range(3))]
        tmpf = carve(4 * 512 * 4).rearrange("p (i n) -> p i n", i=4)
        sqb = carve(2 * 512 * 2).bitcast(BF16).rearrange("p (i n) -> p i n", i=2)
        PTb = carve(2 * 512 * 2).bitcast(BF16).rearrange("p (i n) -> p i n", i=2)
        rcb = carve(3 * 256 * 4).rearrange("p (i n) -> p i n", i=3)
        sinkexp = carve(256 * 4)
        vecs = carve(NVEC * 4)
        cTs = carve(16 * 4)
        cact = carve(16 * 2).bitcast(BF16).rearrange("p (c v) -> p c v", v=2)
        ada = carve(5 * 96 * 4).rearrange("p (l o v) -> p l o v", l=5, v=2)
        der = carve(4 * 64 * 4 + 64)
        ones = carve(128 * 2).bitcast(BF16)
        bones = carve(128 * 2).bitcast(BF16)
        zer = carve(64 * 4)
        spT = carve(256 * 4).rearrange("p (i c r) -> p i c r", i=2, c=8)
        poolout = carve(2 * 8 * 32 * 4).rearrange("p (i c r) -> p i c r", i=2, c=8)
        knout = carve(4 * 144 * 4).rearrange("p (k c) -> p k c", k=4)
        vnout = carve(2 * 256 * 4).rearrange("p (i n) -> p i n", i=2)
        ckT = carve(512 * 2).bitcast(BF16).rearrange("p (k n) -> p k n", k=4)
        print('arena words used', off[0], 'of', NW)
        assert off[0] <= NW, off[0]

        ps = [es.enter_context(nc.psum_tensor("ps%d" % i, [128, 512], F32)) for i in range(8)]
        bank_ctr = {"A": 0, "B": 0, "O": 0, "O3": 0, "S": 0, "OW": 0, "A4": 0}
        bank_map = {"A": [0, 1], "B": [2, 3], "O": [4, 5], "O3": [4, 5, 6, 7], "S": [6, 7], "OW": [4, 5, 0, 1, 2, 3], "A4": [0, 1, 2, 3]}

        def bank(role):
            lst = bank_map[role]
            b = lst[bank_ctr[role] % len(lst)]
            bank_ctr[role] += 1
            return b

        def dG1(i):
            return der[:, i * 64:i * 64 + 16].rearrange("p (c v) -> p c v", v=2)

        def dG2(i):
            return der[:, i * 64 + 16:i * 64 + 32].rearrange("p (c v) -> p c v", v=2)

        def dGP(i):
            return der[:, i * 64 + 32:i * 64 + 48].rearrange("p (c v) -> p c v", v=2)

        def dG1h(i):
            return der[:, i * 64 + 48:i * 64 + 56]

        def dS1h(i):
            return der[:, i * 64 + 56:i * 64 + 64]

        dGkv = der[:, 256:272].rearrange("p (c v) -> p c v", v=2)
        eps_col = vecs[:, V_EPS:V_EPS + 1]
        flag_col = vecs[:, V_FLAG:V_FLAG + 1]

        def adav(i, q):
            return ada[:, i, q * 8:(q + 1) * 8, :]

        slot_i = [0]

        SLOT_DESCS.clear()

        def next_slot(desc):
            k = slot_i[0]
            slot_i[0] += 1
            SLOT_DESCS.append(desc)
            s = k % NR
            S.add("pool", lambda e, k=k, s=s: e.dma_start(out=ring[:, s, :], in_=ws_d[k]),
                  w=[("ring", s)], dma="ring%d" % s)
            return s

        xT_dv = xT_d.rearrange("(c p) t -> p c t", p=128)
        for kc in range(KC):
            S.add("sp", lambda e, kc=kc: e.dma_start(out=xT[:, kc, :], in_=xT_dv[:, kc, :]),
                  w=segkeys("x", 0, T, kc), dma="ldx%d" % kc)
        S.add("sp", lambda e: e.dma_start(out=vecs, in_=vecs_d), w=[("vecs",)], dma="ldvec")
        S.add("sp", lambda e: e.dma_start(out=cTs, in_=cT_d), w=[("cTs",)], dma="ldc")
        S.add("sp", lambda e: e.dma_start(out=spT.rearrange("p i c r -> p (i c r)"), in_=spT_d),
              w=[("spT",)], dma="ldsp")
        S.add("pool", lambda e: e.dma_start(out=ckT.rearrange("p k n -> p (k n)"), in_=ckT_d),
              w=[("ckT",)], dma="ldck")
        ckT_dv = ckT_d.rearrange("p (k n) -> p k n", k=4)
        S.add("sp", lambda e: e.dma_start(out=ksT_d.rearrange("k d c -> d k c")[:, :, 0:112],
                                          in_=ckT_dv[0:64, :, 16:128]), dma="o_ks0")
        S.add("sp", lambda e: e.dma_start(out=vs_d[0:112, :], in_=cvraw_d[16:128, :]), dma="o_vs0")

        import os
        kpro = int(os.environ.get("KPRO", "9"))
        S.enabled = kpro >= 1
        S.add("dve", lambda e: e.memset(ones, 1.0), w=[("ones",)])
        S.add("dve", lambda e: e.memset(bones, 0.0), w=[("bones",)])
        S.add("dve", lambda e: e.memset(bones[0:64, 0:64], 1.0), w=[("bones",)])
        S.add("dve", lambda e: e.memset(bones[64:128, 64:128], 1.0), w=[("bones",)])
        S.add("dve", lambda e: e.memset(zer, 0.0), w=[("zer",)])
        S.add("act", lambda e: e.activation(out=cact.rearrange("p c v -> p (c v)"), in_=cTs, func=AF.Silu),
              r=[("cTs",)], w=[("cact",)])

        bz = 7
        DERc = [None]

        def DERL(i):
            return [("der", i), ("ada", i), ("vecs",)]

        def ada_job(li, obp):
            def run():
                s = next_slot(("ada", li, obp))
                for o in range(2):
                    ob = 2 * obp + o
                    colbase = li * 96 + ob * 2
                    for kc in range(KC):
                        S.add("pe", lambda e, s=s, o=o, kc=kc, cb=colbase: e.matmul(
                            ps[bz][:, cb:cb + 2], lhsT=ring[:, s, (o * 8 + kc) * 128:(o * 8 + kc + 1) * 128],
                            rhs=cact[:, kc, :], start=(kc == 0), stop=(kc == 7)),
                            r=[("ring", s), ("cact",)], w=[("ps", bz)])
            return run

        def ada_finish(li, part=None):
            def run():
                nob = 48 if li < 4 else 16
                boff = V_BADA + li * 48 if li < 4 else V_BKV
                o0, o1 = (0, nob) if part is None else ((0, 24) if part == 0 else (24, 48))
                for v in range(2):
                    S.add("dve", lambda e, v=v: e.tensor_tensor(
                        out=ada[:, li, o0:o1, v],
                        in0=ps[bz][:, li * 96 + 2 * o0:li * 96 + 2 * o1].rearrange("p (o v) -> p o v", v=2)[:, :, v],
                        in1=vecs[:, boff + o0:boff + o1], op=ALU.add),
                        r=[("ps", bz), ("vecs",)], w=[("ada", li)])
                if li == 4:
                    gkv = vecs[:, V_GKV:V_GKV + 8]
                    for v in range(2):
                        S.add("dve", lambda e, v=v: e.scalar_tensor_tensor(
                            out=dGkv[:, :, v], in0=ada[:, 4, 8:16, v], scalar=1.0, in1=gkv, op0=ALU.add, op1=ALU.mult),
                            r=[("ada", 4), ("vecs",)], w=[("der", 4)])
                    return
                i = li
                gm = vecs[:, V_GMIX + i * 8:V_GMIX + i * 8 + 8]
                gf = vecs[:, V_GFFN + i * 8:V_GFFN + i * 8 + 8]
                for v in range(2):
                    if part in (None, 0):
                        S.add("dve", lambda e, v=v: e.scalar_tensor_tensor(
                            out=dG1(i)[:, :, v], in0=adav(i, 1)[:, :, v], scalar=1.0, in1=gm, op0=ALU.add, op1=ALU.mult),
                            r=[("ada", i), ("vecs",)], w=[("der", i)])
                    if part in (None, 1):
                        S.add("dve", lambda e, v=v: e.scalar_tensor_tensor(
                            out=dG2(i)[:, :, v], in0=adav(i, 4)[:, :, v], scalar=1.0, in1=gf, op0=ALU.add, op1=ALU.mult),
                            r=[("ada", i), ("vecs",)], w=[("der", i)])
                    if i < 2 and part in (None, 0):
                        psc = vecs[:, V_PSC + i * 8:V_PSC + i * 8 + 8]
                        S.add("dve", lambda e, v=v, psc=psc: e.tensor_tensor(
                            out=dGP(i)[:, :, v], in0=adav(i, 2)[:, :, v], in1=psc, op=ALU.mult),
                            r=[("ada", i), ("vecs",)], w=[("der", i)])
                if i < 2 and part in (None, 0):
                    S.add("dve", lambda e: e.tensor_scalar(
                        out=dG1h(i), in0=dG1(i)[:, :, 0], scalar1=flag_col, scalar2=None, op0=ALU.mult),
                        r=[("der", i), ("vecs",)], w=[("der", i)])
                    S.add("dve", lambda e: e.tensor_scalar(
                        out=dS1h(i), in0=adav(i, 0)[:, :, 0], scalar1=flag_col, scalar2=None, op0=ALU.mult),
                        r=[("ada", i), ("vecs",)], w=[("der", i)])
            return run

        pending = []

        def queue_ada(li):
            nob = 48 if li < 4 else 16
            for obp in range(nob // 2):
                pending.append(ada_job(li, obp))
            pending.append(ada_finish(li))

        def drain(n):
            for _ in range(min(n, len(pending))):
                pending.pop(0)()


        sq_ctr = [0]

        def norm_stats(blocks, alt=False):
            for (c0, c1) in blocks:
                N = c1 - c0
                b = bank("S")
                tb = 2 + (sq_ctr[0] % 2)
                sq_ctr[0] += 1
                for kc in range(KC):
                    sb = kc % 4
                    sbuf = sqb[:, sb, 0:N] if sb < 2 else PTb[:, sb - 2, 0:N]
                    skey = ("sqb", sb) if sb < 2 else ("PT", sb - 2)
                    if alt and kc % 2 == 1:
                        S.add("dve", lambda e, kc=kc, sbuf=sbuf, c0=c0, c1=c1: e.tensor_tensor(
                            out=sbuf, in0=xT[:, kc, c0:c1], in1=xT[:, kc, c0:c1], op=ALU.mult),
                            r=segkeys("x", c0, c1, kc), w=[skey])
                    else:
                        S.add("act", lambda e, kc=kc, sbuf=sbuf, c0=c0, c1=c1: e.activation(
                            out=sbuf, in_=xT[:, kc, c0:c1], func=AF.Square),
                            r=segkeys("x", c0, c1, kc), w=[skey])
                    S.add("pe", lambda e, kc=kc, sbuf=sbuf, N=N, b=b: e.matmul(
                        ps[b][:, 0:N], lhsT=ones, rhs=sbuf, start=(kc == 0), stop=(kc == 7)),
                        r=[skey, ("ones",)], w=[("ps", b)])
                S.add("act", lambda e, N=N, b=b, tb=tb: e.activation(
                    out=tmpf[:, tb, 0:N], in_=ps[b][:, 0:N], func=AF.Ln, bias=eps_col, scale=1.0 / D),
                    r=[("ps", b), ("vecs",)], w=[("tmp", tb)])
                S.add("act", lambda e, N=N, c0=c0, c1=c1, tb=tb: e.activation(
                    out=rstd[:, c0:c1], in_=tmpf[:, tb, 0:N], func=AF.Exp, scale=-0.5),
                    r=[("tmp", tb)], w=segkeys("rstd", c0, c1))

        ucnt = [0]

        def norm_apply(blocks, Gv, Sv, out):
            for (c0, c1) in blocks:
                for kc in range(KC):
                    for (a, b_, v) in colsegs(c0, c1):
                        n = b_ - a
                        ti = ucnt[0] % 2
                        ucnt[0] += 1
                        S.add("dve", lambda e, kc=kc, a=a, b_=b_, n=n, ti=ti: e.tensor_tensor(
                            out=tmpf[:, ti, 0:n], in0=xT[:, kc, a:b_], in1=rstd[:, a:b_], op=ALU.mult),
                            r=segkeys("x", a, b_, kc) + segkeys("rstd", a, b_), w=[("tmp", ti)])
                        S.add("act", lambda e, kc=kc, a=a, b_=b_, n=n, ti=ti, v=v: e.activation(
                            out=out[:, kc, a:b_], in_=tmpf[:, ti, 0:n], func=AF.Identity,
                            bias=Sv[:, kc, v:v + 1], scale=Gv[:, kc, v:v + 1]),
                            r=[("tmp", ti)] + DERc[0], w=segkeys("R1", a, b_, kc))

        def norm_full(blocks, Gv, Sv, out):
            for bi, blk in enumerate(blocks):
                norm_stats([blk], alt=True)
                if bi >= 1:
                    norm_apply([blocks[bi - 1]], Gv, Sv, out)
            norm_apply([blocks[-1]], Gv, Sv, out)

        hn_ctr = [0]

        def head_norm(b, N, gcol, dest, dest_keys, caps=(), sbank=None):
            k = hn_ctr[0] % 2
            hn_ctr[0] += 1
            sb, tb, bs = k, 2 + k, (bank("S") if sbank is None else sbank)
            S.add("act", lambda e: e.activation(out=sqb[:, sb, 0:N], in_=ps[b][:, 0:N], func=AF.Square),
                  r=[("ps", b)], w=[("sqb", sb)])
            S.add("pe", lambda e: e.matmul(ps[bs][:, 0:N], lhsT=bones, rhs=sqb[:, sb, 0:N], start=True, stop=True),
                  r=[("sqb", sb), ("bones",)], w=[("ps", bs)])
            S.add("act", lambda e: e.activation(
                out=tmpf[:, tb, 0:N], in_=ps[bs][:, 0:N], func=AF.Ln, bias=eps_col, scale=1.0 / 64),
                r=[("ps", bs), ("vecs",)], w=[("tmp", tb)])
            S.add("act", lambda e: e.activation(
                out=tmpf[:, tb, 0:N], in_=tmpf[:, tb, 0:N], func=AF.Exp, scale=-0.5),
                r=[("tmp", tb)], w=[("tmp", tb)])
            S.add("dve", lambda e: e.scalar_tensor_tensor(
                out=dest, in0=ps[b][:, 0:N], scalar=gcol, in1=tmpf[:, tb, 0:N], op0=ALU.mult, op1=ALU.mult),
                r=[("ps", b), ("tmp", tb), ("vecs",)], w=dest_keys)
            for (pa, pb_, cdst, ckeys) in caps:
                S.add("dve", lambda e, pa=pa, pb_=pb_, cdst=cdst: e.scalar_tensor_tensor(
                    out=cdst, in0=ps[b][0:64, pa:pb_], scalar=gcol[0:64, :], in1=tmpf[0:64, tb, pa:pb_],
                    op0=ALU.mult, op1=ALU.mult),
                    r=[("ps", b), ("tmp", tb), ("vecs",)], w=ckeys)

        def x_update(b, blk, m, gvec):
            c0, c1 = blk
            for (a, b_, v) in colsegs(c0, c1):
                S.add("dve", lambda e, b=b, m=m, a=a, b_=b_, v=v, c0=c0: e.scalar_tensor_tensor(
                    out=xT[:, m, a:b_], in0=ps[b][:, a - c0:b_ - c0], scalar=gvec[:, m, v:v + 1],
                    in1=xT[:, m, a:b_], op0=ALU.mult, op1=ALU.add),
                    r=[("ps", b)] + DERc[0] + segkeys("x", a, b_, m), w=segkeys("x", a, b_, m))

        def ffn(i, blocks):
            norm_full(blocks, dG2(i), adav(i, 3), R1)
            g2 = adav(i, 5)
            sgc = [0]
            for (j0, j1) in GROUPS:
                for jj, j in enumerate(range(j0, j1)):
                    s = next_slot(("ffn_in", i, j))
                    for (c0, c1) in blocks:
                        N = c1 - c0
                        bg = bank("A")
                        bu = bank("B")
                        for kc in range(KC):
                            S.add("pe", lambda e, s=s, kc=kc, c0=c0, c1=c1, N=N, bg=bg: e.matmul(
                                ps[bg][:, 0:N], lhsT=ring[:, s, kc * 128:(kc + 1) * 128], rhs=R1[:, kc, c0:c1],
                                start=(kc == 0), stop=(kc == 7)),
                                r=[("ring", s)] + segkeys("R1", c0, c1, kc), w=[("ps", bg)])
                        for kc in range(KC):
                            S.add("pe", lambda e, s=s, kc=kc, c0=c0, c1=c1, N=N, bu=bu: e.matmul(
                                ps[bu][:, 0:N], lhsT=ring[:, s, (8 + kc) * 128:(9 + kc) * 128], rhs=R1[:, kc, c0:c1],
                                start=(kc == 0), stop=(kc == 7)),
                                r=[("ring", s)] + segkeys("R1", c0, c1, kc), w=[("ps", bu)])
                        ti = sgc[0] % 2
                        sgc[0] += 1
                        S.add("act", lambda e, bg=bg, N=N, ti=ti: e.activation(
                            out=tmpf[:, ti, 0:N], in_=ps[bg][:, 0:N], func=AF.Silu),
                            r=[("ps", bg)], w=[("tmp", ti)])
                        S.add("dve", lambda e, bu=bu, N=N, ti=ti, jj=jj, c0=c0, c1=c1: e.tensor_tensor(
                            out=X4[:, jj, c0:c1], in0=tmpf[:, ti, 0:N], in1=ps[bu][:, 0:N], op=ALU.mult),
                            r=[("ps", bu), ("tmp", ti)], w=segkeys("X4", c0, c1, jj))
                    drain(2)
                nj = j1 - j0
                oslots = [next_slot(("ffn_out", i, j0 + 2 * q_)) for q_ in range(nj // 2)]
                for (c0, c1) in blocks:
                    N = c1 - c0
                    for m in range(KC):
                        bo = bank("OW")
                        for jj in range(nj):
                            s = oslots[jj // 2]
                            o = jj % 2
                            S.add("pe", lambda e, s=s, o=o, m=m, jj=jj, c0=c0, c1=c1, N=N, bo=bo, nj=nj: e.matmul(
                                ps[bo][:, 0:N], lhsT=ring[:, s, o * 1024 + m * 128:o * 1024 + (m + 1) * 128],
                                rhs=X4[:, jj, c0:c1], start=(jj == 0), stop=(jj == nj - 1)),
                                r=[("ring", s)] + segkeys("X4", c0, c1, jj), w=[("ps", bo)])
                        x_update(bo, (c0, c1), m, g2)
            drain(len(pending))

        def ptc(t):
            return 16 + t if t < 16 else 32 + t

        def layer_a(i):
            DERc[0] = DERL(i)
            norm_stats(BLK_A)
            if i == 0:
                for obp in range(12):
                    pending.append(ada_job(0, obp))
                pending.append(ada_finish(0, part=0))
                for obp in range(12, 24):
                    pending.append(ada_job(0, obp))
                drain(len(pending))
                pending.append(ada_finish(0, part=1))
            wps = next_slot(("pool", i))
            G1, S1, G1h, S1h, GP = dG1(i), adav(i, 0), dG1h(i), dS1h(i), dGP(i)

            def stage1(kc):
                k = kc % 2
                PA_, PB_, PC_ = PSETS[k]
                ka, kc_ = ("pa", k), ("pc", k)
                S.add("dve", lambda e: e.tensor_copy(out=PA_[:, 0:16], in_=spT[:, i, kc, :]), r=[("spT",)], w=[ka])
                S.add("dve", lambda e: e.memset(PA_[:, 32:48], 0.0), w=[ka])
                S.add("pool", lambda e: e.tensor_tensor(
                    out=PC_[:, 16:32], in0=xT[:, kc, 0:16], in1=rstd[:, 0:16], op=ALU.mult),
                    r=segkeys("x", 0, 16, kc) + segkeys("rstd", 0, 16), w=[kc_])
                S.add("pool", lambda e: e.tensor_tensor(
                    out=PC_[:, 48:PTL], in0=xT[:, kc, 16:T], in1=rstd[:, 16:T], op=ALU.mult),
                    r=segkeys("x", 16, T, kc) + segkeys("rstd", 16, T), w=[kc_])
                S.add("act", lambda e: e.activation(
                    out=PA_[:, 16:32], in_=PC_[:, 16:32], func=AF.Identity,
                    bias=S1[:, kc, 1:2], scale=G1[:, kc, 1:2]), r=[kc_] + DERc[0], w=[ka])
                S.add("act", lambda e: e.activation(
                    out=PA_[:, 48:208], in_=PC_[:, 48:208], func=AF.Identity,
                    bias=S1h[:, kc:kc + 1], scale=G1h[:, kc:kc + 1]), r=[kc_] + DERc[0], w=[ka])
                S.add("act", lambda e: e.activation(
                    out=PA_[:, 208:PTL], in_=PC_[:, 208:PTL], func=AF.Identity,
                    bias=S1[:, kc, 0:1], scale=G1[:, kc, 0:1]), r=[kc_] + DERc[0], w=[ka])
                S.add("act", lambda e: e.activation(
                    out=poolout[:, i, kc, 0:16], in_=PA_[:, 16:32], func=AF.Identity), r=[ka], w=[("poolout",)])
                S.add("act", lambda e: e.activation(
                    out=poolout[:, i, kc, 16:32], in_=PA_[:, PTL - 16:PTL], func=AF.Identity), r=[ka], w=[("poolout",)])

            def stage2(kc):
                k = kc % 2
                g = kc // 2
                PA_, PB_, PC_ = PSETS[k]
                ka = ("pa", k)
                dslot = 2 * (g % 2) + (kc % 2)
                seq = [(PA_, ("pa", k)), (PB_, ("pb", k)), (PC_, ("pc", k)), (PB_, ("pb", k)), (PC_, ("pc", k))]
                cur, curk = seq[0]
                for L in range(1, g + 2):
                    st = 2 ** (L - 1)
                    lo = 2 ** L - 1
                    dst, dstk = seq[L]
                    S.add("dve", lambda e, cur=cur, dst=dst, st=st, lo=lo: e.tensor_tensor(
                        out=dst[:, lo:PTL], in0=cur[:, lo:PTL], in1=cur[:, lo - st:PTL - st], op=ALU.add),
                        r=[curk], w=[dstk])
                    cur, curk = dst, dstk
                inv = 1.0 / (2 ** (g + 1))
                S.add("dve", lambda e: e.scalar_tensor_tensor(
                    out=X4[:, dslot, 0:16], in0=cur[:, 16:32], scalar=inv, in1=PA_[:, 16:32],
                    op0=ALU.mult, op1=ALU.subtract), r=[curk, ka], w=segkeys("X4", 0, 16, dslot))
                S.add("dve", lambda e: e.scalar_tensor_tensor(
                    out=X4[:, dslot, 16:T], in0=cur[:, 48:PTL], scalar=inv, in1=PA_[:, 48:PTL],
                    op0=ALU.mult, op1=ALU.subtract), r=[curk, ka], w=segkeys("X4", 16, T, dslot))
                S.add("dve", lambda e: e.tensor_tensor(
                    out=tmpf[:, 3, 0:16], in0=cur[:, 208:224], in1=vecs[:, V_INVC + kc * 16:V_INVC + kc * 16 + 16],
                    op=ALU.mult), r=[curk, ("vecs",)], w=[("tmp", 3)])
                S.add("dve", lambda e: e.tensor_tensor(
                    out=X4[:, dslot, 176:192], in0=tmpf[:, 3, 0:16], in1=PA_[:, 208:224], op=ALU.subtract),
                    r=[("tmp", 3), ka], w=segkeys("X4", 176, 192, dslot))

            def group_out(g):
                for ec in range(2):
                    m = 2 * g + ec
                    for (c0, c1) in BLK_A:
                        N = c1 - c0
                        bo = bank("OW")
                        for kk in range(2):
                            dslot = 2 * (g % 2) + kk
                            wo = ((g * 2 + ec) * 2 + kk) * 128
                            S.add("pe", lambda e, wo=wo, dslot=dslot, kk=kk, c0=c0, c1=c1, N=N, bo=bo: e.matmul(
                                ps[bo][:, 0:N], lhsT=ring[:, wps, wo:wo + 128], rhs=X4[:, dslot, c0:c1],
                                start=(kk == 0), stop=(kk == 1)),
                                r=[("ring", wps)] + segkeys("X4", c0, c1, dslot), w=[("ps", bo)])
                        x_update(bo, (c0, c1), m, GP)

            stage1(0)
            for kc in range(KC):
                if kc + 1 < KC:
                    stage1(kc + 1)
                stage2(kc)
                if kc % 2 == 1:
                    if i == 0 and kc == 3:
                        drain(1)
                    group_out(kc // 2)
            drain(len(pending))
            if i == 0:
                queue_ada(1)
                queue_ada(4)
            else:
                queue_ada(2)
            ffn(i, BLK_A)

        def kv_phase():
            DERc[0] = DERL(4)
            alias = [("pa", 1), ("pb", 1), ("pc", 1)]
            S.add("dve", lambda e: e.memset(Vt[:, 0:17, 64:128], 1.0), w=[("V", j) for j in range(17)] + alias)
            S.add("dve", lambda e: e.memset(Vt[:, 0:17, 256:320], 1.0), w=[("V", j) for j in range(17)] + alias)
            S.add("dve", lambda e: e.memset(Vt[0:32, 17, :], 0.0), w=[("V", 17)] + alias)
            S.add("dve", lambda e: e.memset(Vt[0:16, 17, 64:128], 1.0), w=[("V", 17)] + alias)
            S.add("dve", lambda e: e.memset(Vt[0:16, 17, 256:320], 1.0), w=[("V", 17)] + alias)
            S.add("pool", lambda e: e.dma_start(out=Vt[:, 18, :], in_=cv_d), w=[("V", 18)] + alias, dma="ldcv")
            norm_full(BLK_A, dGkv, ada[:, 4, 0:8, :], R1)
            gk = vecs[:, V_GK:V_GK + 1]
            ks = [next_slot(("wk", 0)), next_slot(("wk", 1))]
            vsl = next_slot(("wv",))
            def v_job(j, t0, nt, idx):
                bz_ = 4 + (idx % 2)
                mt = max(nt, 32)
                for kc in range(KC):
                    S.add("pe", lambda e, kc=kc: e.matmul(
                        ps[bz_][0:mt, 0:256], lhsT=R1[:, kc, t0:t0 + mt], rhs=ring[:, vsl, kc * 256:(kc + 1) * 256],
                        start=(kc == 0), stop=(kc == 7)),
                        r=[("ring", vsl)] + segkeys("R1", t0, t0 + mt, kc), w=[("ps", bz_)])
                pv4 = ps[bz_][0:nt, 0:256].rearrange("p (k d) -> p k d", k=4)
                S.add("dve", lambda e: e.tensor_copy(
                    out=Vt[0:nt, j, 0:192].rearrange("p (k d) -> p k d", d=64)[:, 0:3:2, :], in_=pv4[:, 0:2, :]),
                    r=[("ps", bz_)], w=[("V", j)])
                S.add("dve", lambda e: e.tensor_copy(
                    out=Vt[0:nt, j, 192:384].rearrange("p (k d) -> p k d", d=64)[:, 0:3:2, :], in_=pv4[:, 2:4, :]),
                    r=[("ps", bz_)], w=[("V", j)])
                if j == 16:
                    S.add("dve", lambda e: e.tensor_copy(out=vnout[:, 0, :], in_=ps[bz_][:, 0:256]),
                          r=[("ps", bz_)], w=[("vnout",)])
                if j == 17:
                    S.add("dve", lambda e: e.tensor_copy(out=vnout[0:16, 1, :], in_=ps[bz_][0:16, 0:256]),
                          r=[("ps", bz_)], w=[("vnout",)])
                if j == 0:
                    S.add("dve", lambda e: e.tensor_scalar(
                        out=Vt[:, 0, :], in0=Vt[:, 0, :], scalar1=flag_col, scalar2=None, op0=ALU.mult),
                        r=[("V", 0), ("vecs",)], w=[("V", 0)])

            vjobs = [(17, 0, 16, 0)] + [(j, 48 + 128 * j, 128, j + 1) for j in range(17)]
            for kvh in range(4):
                s = ks[kvh // 2]
                base = (kvh % 2) * 1024
                for (c0, c1) in BLK_A:
                    N = c1 - c0
                    b = bank("A4")
                    for kc in range(KC):
                        S.add("pe", lambda e, s=s, base=base, kc=kc, c0=c0, c1=c1, N=N, b=b: e.matmul(
                            ps[b][:, 0:N], lhsT=ring[:, s, base + kc * 128:base + (kc + 1) * 128],
                            rhs=R1[:, kc, c0:c1], start=(kc == 0), stop=(kc == 7)),
                            r=[("ring", s)] + segkeys("R1", c0, c1, kc), w=[("ps", b)])
                    caps = []
                    if c0 == 0:
                        caps.append((0, 16, knout[0:64, kvh, 0:16], [("knout",)]))
                    if c1 == T:
                        caps.append((N - 128, N, knout[0:64, kvh, 16:144], [("knout",)]))
                    head_norm(b, N, gk, kT[:, kvh, c0:c1], segkeys("kT", c0, c1, kvh), caps)
                    if vjobs:
                        v_job(*vjobs.pop(0))
            while vjobs:
                v_job(*vjobs.pop(0))
            S.add("sp", lambda e: e.dma_start(out=knT_d.rearrange("k d c -> d k c"), in_=knout[0:64, :, :]),
                  r=[("knout",)], dma="o_kn")
            S.add("sp", lambda e: e.dma_start(out=ksT_d.rearrange("k d c -> d k c")[:, :, 112:128],
                                              in_=knout[0:64, :, 0:16]), r=[("knout",)], dma="o_ks1")
            S.add("sp", lambda e: e.dma_start(out=vn_d[16:144, :], in_=vnout[:, 0, :]), r=[("vnout",)], dma="o_vn0")
            S.add("sp", lambda e: e.dma_start(out=vn_d[0:16, :], in_=vnout[0:16, 1, :]), r=[("vnout",)], dma="o_vn1")
            S.add("sp", lambda e: e.dma_start(out=vs_d[112:128, :], in_=vnout[0:16, 1, :]), r=[("vnout",)], dma="o_vs1")
            S.add("sp", lambda e: e.dma_start(out=poolT_d.rearrange("i (c p) r -> p i c r", p=128), in_=poolout),
                  r=[("poolout",)], dma="o_pool")

        VC0 = [0, 64, 192, 256]

        def layer_b(i):
            jl = i - 2
            DERc[0] = DERL(i)
            if i == 2:
                queue_ada(3)
            if i == 2:
                norm_apply(BLK_B, dG1(i), adav(i, 0), R1)
            else:
                norm_full(BLK_B, dG1(i), adav(i, 0), R1)
            g1 = adav(i, 2)
            gq = vecs[:, V_GQ + jl:V_GQ + jl + 1]
            uc = [0]
            rq = rstd.bitcast(BF16).rearrange("p (c t) -> p c t", c=2)
            Qb = [X4, rq]

            def qkeys(qi, c0, c1, o):
                return segkeys("X4", c0, c1, o) if qi == 0 else segkeys("rstd", c0, c1)

            def qproj_jobs(kvh, inline):
                state = {}
                jobs = []

                def mk(o, c0, c1):
                    def run():
                        if "qs" not in state:
                            state["qs"] = next_slot(("wq", jl, kvh))
                        qs = state["qs"]
                        N = c1 - c0
                        b = 7 if inline else bank("A4")
                        for kc in range(KC):
                            S.add("pe", lambda e, kc=kc: e.matmul(
                                ps[b][:, 0:N], lhsT=ring[:, qs, (o * 8 + kc) * 128:(o * 8 + kc + 1) * 128],
                                rhs=R1[:, kc, c0:c1], start=(kc == 0), stop=(kc == 7)),
                                r=[("ring", qs)] + segkeys("R1", c0, c1, kc), w=[("ps", b)])
                        head_norm(b, N, gq, rq[:, o, c0:c1], qkeys(1, c0, c1, o),
                                  sbank=(6 if inline else None))
                    return run
                for o in range(2):
                    for (c0, c1) in BLK_B:
                        jobs.append(mk(o, c0, c1))
                return jobs

            qdone = set()
            for kvh in range(4):
                if kvh not in qdone:
                    for jb in qproj_jobs(kvh, False):
                        jb()
                for blk in range(4):
                    par, o = blk // 2, blk % 2
                    h = kvh * 4 + 2 * o + par
                    S.add("act", lambda e, blk=blk, h=h: e.activation(
                        out=sinkexp[:, blk * 64:(blk + 1) * 64], in_=zer, func=AF.Exp,
                        bias=vecs[:, V_SINK + jl * 16 + h:V_SINK + jl * 16 + h + 1], scale=1.0),
                        r=[("zer",), ("vecs",)], w=[("sinkexp",)])
                nxtq = []
                Qc = rq
                qi = 1
                a0 = 2 * (kvh % 2)
                nb = 0 if kvh % 2 == 0 else 64
                db = 64 - nb
                vc = VC0[kvh]
                units = []
                units.append((0, 16,
                              (lambda hb, kvh=kvh: ckT[hb:hb + 64, kvh, 0:128], 18, [("ckT",)]),
                              (lambda hb, kvh=kvh: kT[hb:hb + 64, kvh, 0:32], 32, 0, 32, 17, segkeys("kT", 0, 32, kvh))))
                for c in range(32):
                    if c % 2 == 0:
                        jf, jh, hh = c // 2, c // 2 + 1, 0
                    else:
                        jf, jh, hh = (c + 1) // 2, (c - 1) // 2, 1
                    cf, ch = 48 + 128 * jf, 48 + 128 * jh
                    units.append((C_OWN + 64 * c, 64,
                                  (lambda hb, cf=cf, kvh=kvh: kT[hb:hb + 64, kvh, cf:cf + 128], jf,
                                   segkeys("kT", cf, cf + 128, kvh)),
                                  (lambda hb, ch=ch, kvh=kvh: kT[hb:hb + 64, kvh, ch:ch + 128], 128, 64 * hh, 64 * hh + 64,
                                   jh, segkeys("kT", ch, ch + 128, kvh))))

                def emit_front(u, ui):
                    q0, nq, full, part = u
                    bpar = [bank("A"), bank("B")]
                    pt = ui % 2
                    for fh in range(2):
                        src = full if fh == 0 else part
                        M = 128 if fh == 0 else part[1]
                        keys = full[2] if fh == 0 else part[5]
                        for o in range(2):
                            for par in range(2):
                                hb = par * 64
                                bb = bpar[par]
                                cbase = fh * 256 + o * nq
                                S.add("pe", lambda e, o=o, hb=hb, q0=q0, nq=nq, bb=bb, src=src, M=M, cbase=cbase, Qc=Qc: e.matmul(
                                    ps[bb][0:M, cbase:cbase + nq], lhsT=src[0](hb), rhs=Qc[hb:hb + 64, o, q0:q0 + nq],
                                    start=True, stop=True),
                                    r=keys + qkeys(qi, q0, q0 + nq, o), w=[("ps", bb)])
                    for par in range(2):
                        bb = bpar[par]
                        S.add("act", lambda e, nq=nq, bb=bb, pt=pt, par=par: e.activation(
                            out=PTb[:, pt, par * 256:(par + 1) * 256].rearrange("p (f x) -> p f x", f=2)[:, :, 0:2 * nq],
                            in_=ps[bb][:, 0:512].rearrange("p (f x) -> p f x", f=2)[:, :, 0:2 * nq],
                            func=AF.Exp, scale=0.125),
                            r=[("ps", bb)], w=[("PT", pt)])
                    return (u, pt)

                def emit_B(st):
                    u, pt = st
                    q0, nq, full, part = u
                    W = 4 * nq
                    bo = bank("O3")
                    r0, r1_ = part[2], part[3]
                    S.add("pe", lambda e, W=W, nq=nq, bo=bo, pt=pt, full=full, vc=vc: e.matmul(
                        ps[bo][:, 0:W].rearrange("p (a x) -> p a x", a=2), lhsT=Vt[:, full[1], vc:vc + 128],
                        rhs=PTb[:, pt, :].rearrange("p (a f x) -> p a f x", a=2, f=2)[:, :, 0, 0:2 * nq],
                        start=True, stop=False),
                        r=[("V", full[1]), ("PT", pt)], w=[("ps", bo)])
                    S.add("pe", lambda e, W=W, nq=nq, bo=bo, pt=pt, part=part, r0=r0, r1_=r1_, vc=vc: e.matmul(
                        ps[bo][:, 0:W].rearrange("p (a x) -> p a x", a=2), lhsT=Vt[r0:r1_, part[4], vc:vc + 128],
                        rhs=PTb[r0:r1_, pt, :].rearrange("p (a f x) -> p a f x", a=2, f=2)[:, :, 1, 0:2 * nq],
                        start=False, stop=True),
                        r=[("V", part[4]), ("PT", pt)], w=[("ps", bo)])
                    ri = uc[0] % 6
                    uc[0] += 1
                    rc = rcb[64 * (ri // 3):64 * (ri // 3) + 64, ri % 3, 0:W]
                    S.add("dve", lambda e, W=W, nq=nq, bo=bo, rc=rc, db=db: e.tensor_tensor(
                        out=rc.rearrange("p (g q) -> p g q", g=4),
                        in0=ps[bo][db:db + 64, 0:W].rearrange("p (g q) -> p g q", g=4),
                        in1=sinkexp[db:db + 64, :].rearrange("p (g q) -> p g q", g=4)[:, :, 0:nq], op=ALU.add),
                        r=[("ps", bo), ("sinkexp",)], w=[("rcb", ri)])
                    return (u, bo, ri, rc)

                def emit_C1(st):
                    u, bo, ri, rc = st
                    S.add("act", lambda e, rc=rc: e.activation(out=rc, in_=rc, func=AF.Ln),
                          r=[("rcb", ri)], w=[("rcb", ri)])
                    return st

                def emit_C2(st):
                    u, bo, ri, rc = st
                    q0, nq, full, part = u
                    S.add("act", lambda e, rc=rc: e.activation(out=rc, in_=rc, func=AF.Exp, scale=-1.0),
                          r=[("rcb", ri)], w=[("rcb", ri)])
                    for par in range(2):
                        S.add("dve", lambda e, nq=nq, bo=bo, rc=rc, par=par, q0=q0, nb=nb, a0=a0: e.tensor_tensor(
                            out=X4[par * 64:par * 64 + 64, a0:a0 + 2, q0:q0 + nq],
                            in0=ps[bo][nb:nb + 64, par * 2 * nq:(par + 1) * 2 * nq].rearrange("p (o q) -> p o q", o=2),
                            in1=rc[:, par * 2 * nq:(par + 1) * 2 * nq].rearrange("p (o q) -> p o q", o=2),
                            op=ALU.mult),
                            r=[("ps", bo), ("rcb", ri)],
                            w=segkeys("X4", q0, q0 + nq, a0) + segkeys("X4", q0, q0 + nq, a0 + 1))

                nun = len(units)
                stA, stB, stC = {}, {}, {}
                for t in range(nun + 4):
                    if 0 <= t - 4 < nun:
                        emit_C2(stC.pop(t - 4))
                    if 0 <= t - 3 < nun:
                        stC[t - 3] = emit_C1(stB.pop(t - 3))
                    if t < nun:
                        stA[t] = emit_front(units[t], t)
                    if 0 <= t - 1 < nun:
                        stB[t - 1] = emit_B(stA.pop(t - 1))
                while nxtq:
                    nxtq.pop(0)()
                if kvh % 2 == 1:
                    osl = [next_slot(("wo", jl, kvh - 1)), next_slot(("wo", jl, kvh))]
                    nxt = qproj_jobs(kvh + 1, False) if kvh + 1 < 4 else []
                    saved = (bank_map["OW"], bank_map["A4"])
                    if nxt:
                        qdone.add(kvh + 1)
                        bank_map["OW"], bank_map["A4"] = [4, 5, 0, 1], [2, 3]
                    cnt_items = 0
                    for (c0, c1) in BLK_B:
                        N = c1 - c0
                        for m in range(KC):
                            bo = bank("OW")
                            for kk in range(2):
                                for o in range(2):
                                    S.add("pe", lambda e, m=m, o=o, kk=kk, c0=c0, c1=c1, N=N, bo=bo, osl=osl: e.matmul(
                                        ps[bo][:, 0:N], lhsT=ring[:, osl[kk], (m * 2 + o) * 128:(m * 2 + o + 1) * 128],
                                        rhs=X4[:, 2 * kk + o, c0:c1], start=(kk == 0 and o == 0), stop=(kk == 1 and o == 1)),
                                        r=[("ring", osl[kk])] + segkeys("X4", c0, c1, 2 * kk + o), w=[("ps", bo)])
                            x_update(bo, (c0, c1), m, g1)
                            cnt_items += 1
                            if cnt_items % 4 == 0 and nxt:
                                nxt.pop(0)()
                    while nxt:
                        nxt.pop(0)()
                    bank_map["OW"], bank_map["A4"] = saved
            ffn(i, BLK_B)

        import os
        stop = os.environ.get("KSTOP", "all")
        order = ["pro", "a0", "a1", "kv", "b2", "all"]
        lvl = order.index(stop)
        if lvl >= 1:
            layer_a(0)
        if lvl >= 2:
            layer_a(1)
        if lvl >= 3:
            kv_phase()
        if lvl >= 4:
            layer_b(2)
        if lvl >= 5:
            layer_b(3)
        yT_dv = yT_d.rearrange("(c p) t -> p c t", p=128)
        for bi, (c0, c1) in enumerate(BLK_B):
            d0 = 0 if c0 == 0 else 16 + (c0 - C_OWN)
            S.add("sp", lambda e, c0=c0, c1=c1, d0=d0: e.dma_start(
                out=yT_dv[:, :, d0:d0 + (c1 - c0)], in_=xT[:, :, c0:c1]),
                r=[k for kc in range(KC) for k in segkeys("x", c0, c1, kc)], dma="o_y%d" % bi)
        assert lvl < 5 or slot_i[0] == N_SLOTS, (slot_i[0], N_SLOTS)
        print('ops', len(S.ops), 'slots', slot_i[0])

        semkeys = S.analyze()
        sems = {sk: es.enter_context(nc.semaphore("s%d" % n)) for n, sk in enumerate(semkeys)}
        block = es.enter_context(nc.Block())

        def emit_engine(engname):
            def body(e):
                waited = {}
                for k, (eng, fn, r, w, dma) in enumerate(S.ops):
                    if eng != engname:
                        continue
                    for d in S.need_all[k]:
                        sk, val = S.event[d]
                        if waited.get(sk, 0) < val:
                            e.wait_ge(sems[sk], val)
                            waited[sk] = val
                    inst = fn(e)
                    if k in S.event:
                        sk, val = S.event[k]
                        inst.then_inc(sems[sk], 16 if dma is not None else 1)
                if engname == "sp":
                    for name, val in S.final_dma.items():
                        if name.startswith("o_"):
                            e.wait_ge(sems[("dma", name)], val)
            return body

        block.tensor(emit_engine("pe"))
        block.scalar(emit_engine("act"))
        block.vector(emit_engine("dve"))
        block.gpsimd(emit_engine("pool"))
        block.sync(emit_engine("sp"))
    return nc


SLOT_DESCS = []
N_SLOTS = 257


def build_wstream(inp):
    assert len(SLOT_DESCS) == N_SLOTS, len(SLOT_DESCS)
    slots = np.zeros((N_SLOTS, 128, 2048), np.float32)
    w_ada = [np.asarray(inp["w_ada"][i]).reshape(8, 128, 48, 128) for i in range(4)]
    w_ada.append(np.asarray(inp["w_ada_kv"]).reshape(8, 128, 16, 128))
    Win = [np.asarray(inp["w_ffn_in"][i]).reshape(8, 128, 2, NJ, 128) for i in range(4)]
    Wout = [np.asarray(inp["w_ffn_out"][i]).reshape(NJ, 128, 1024) for i in range(4)]
    wkv = np.asarray(inp["w_kv"])
    Wk = wkv[:, :256].reshape(8, 128, 4, 64)
    Wkd = np.concatenate([Wk, Wk], axis=3)
    for k, d in enumerate(SLOT_DESCS):
        kind = d[0]
        if kind == "ada":
            _, li, obp = d
            a = w_ada[li][:, :, 2 * obp:2 * obp + 2, :].transpose(1, 2, 0, 3)
        elif kind == "pool":
            wp = np.asarray(inp["w_pool"][d[1]]).reshape(4, 2, 128, 2, 128)
            a = wp.transpose(2, 0, 3, 1, 4)
        elif kind == "ffn_in":
            _, i, j = d
            a = Win[i][:, :, :, j, :].transpose(1, 2, 0, 3)
        elif kind == "ffn_out":
            _, i, jp = d
            a = Wout[i][jp:jp + 2].transpose(1, 0, 2)
        elif kind == "wk":
            h = d[1]
            a = Wkd[:, :, 2 * h:2 * h + 2].transpose(1, 2, 0, 3)
        elif kind == "wv":
            a = wkv[:, 256:].reshape(8, 128, 256).transpose(1, 0, 2)
        elif kind == "wq":
            _, jl, kvh = d
            Wq = np.asarray(inp["w_q"][jl]).reshape(8, 128, 8, 128)
            a = Wq[:, :, 2 * kvh:2 * kvh + 2, :].transpose(1, 2, 0, 3)
        elif kind == "wo":
            _, jl, kvh = d
            Wo = np.asarray(inp["w_o"][jl]).reshape(8, 128, 8, 128)
            a = Wo[2 * kvh:2 * kvh + 2].transpose(1, 2, 0, 3)
        else:
            raise ValueError(d)
        a = a.reshape(128, -1)
        slots[k, :, :a.shape[1]] = a
    return slots


def fm(v):
    v = np.asarray(v, np.float32)
    lead = v.shape[:-1]
    a = v.reshape(lead + (8, 128))
    a = np.moveaxis(a, -1, 0)
    return a.reshape(128, -1)


_CACHE = {}


def prep_inputs(inp):
    if "nc" not in _CACHE:
        _CACHE["nc"] = build_program()
    ws = build_wstream(inp)
    xp, xs = inp["x_prompt"], inp["x_sample"]
    in_maps = []
    for r in range(NCORES):
        pb, s, sb = r // 4, r % 4, r
        xT = np.zeros((D, T), np.float32)
        xT[:, 0:16] = xs[sb].T
        if s > 0:
            xT[:, 16:176] = xp[pb, s * NOWN - 160:s * NOWN].T
        xT[:, 176:] = xp[pb, s * NOWN:(s + 1) * NOWN].T
        cT = np.stack([fm(inp["c_prompt"][pb]), fm(inp["c_sample"][sb])], axis=2).reshape(128, 16)
        vecs = np.zeros((128, NVEC), np.float32)
        for i in range(4):
            vecs[:, V_BADA + i * 48:V_BADA + (i + 1) * 48] = inp["b_ada"][i].reshape(48, 128).T
            vecs[:, V_GMIX + i * 8:V_GMIX + (i + 1) * 8] = fm(inp["g_mix"][i])
            vecs[:, V_GFFN + i * 8:V_GFFN + (i + 1) * 8] = fm(inp["g_ffn"][i])
        for i in range(2):
            vecs[:, V_PSC + i * 8:V_PSC + (i + 1) * 8] = fm(inp["pool_scale"][i])
            vecs[:, V_GQ + i] = np.tile(inp["g_q"][i], 2)
            vecs[:, V_SINK + i * 16:V_SINK + (i + 1) * 16] = inp["sinks"][i][None, :]
        vecs[:, V_GKV:V_GKV + 8] = fm(inp["g_kv"])
        vecs[:, V_BKV:V_BKV + 16] = inp["b_ada_kv"].reshape(16, 128).T
        vecs[:, V_GK] = np.tile(inp["g_k"], 2)
        vecs[:, V_FLAG] = 1.0 if s > 0 else 0.0
        pos = s * NOWN + np.arange(16)
        for kc in range(8):
            w = 2 ** (kc // 2 + 1)
            vecs[:, V_INVC + kc * 16:V_INVC + (kc + 1) * 16] = (1.0 / np.minimum(w, pos + 1))[None, :]
        vecs[:, V_EPS] = EPS
        spT = np.zeros((128, 2, 8, 16), np.float32)
        sp = inp["state_pool"][:, sb]
        spT[:, :, :, 1:] = sp.reshape(2, 15, 8, 128).transpose(3, 0, 2, 1)
        ck = inp["cache_k"][sb]
        ckT = np.tile(ck.transpose(2, 1, 0), (2, 1, 1)).reshape(128, 512)
        cvr = inp["cache_v"][sb].reshape(128, 256)
        cv = np.ones((128, 384), np.float32)
        cv[:, 0:64] = cvr[:, 0:64]
        cv[:, 128:192] = cvr[:, 64:128]
        cv[:, 192:256] = cvr[:, 128:192]
        cv[:, 320:384] = cvr[:, 192:256]
        in_maps.append({"xT": xT, "cT": np.ascontiguousarray(cT), "vecs": vecs,
                        "spT": spT.reshape(128, 256), "ckT": np.ascontiguousarray(ckT), "cv": cv,
                        "cvraw": np.ascontiguousarray(cvr), "wstream": ws})
    return in_maps


def assemble(R, inp):
    xp = inp["x_prompt"]
    B, L = xp.shape[0], xp.shape[1]
    y_p = np.zeros((B, L, D), np.float32)
    y_s = np.zeros((NCORES, 16, D), np.float32)
    pool_p = np.zeros((2, B, 15, D), np.float32)
    k_p = np.zeros((B, 128, 4, 64), np.float32)
    v_p = np.zeros((B, 128, 4, 64), np.float32)
    pool_s = np.zeros((2, NCORES, 15, D), np.float32)
    k_s = np.zeros((NCORES, 128, 4, 64), np.float32)
    v_s = np.zeros((NCORES, 128, 4, 64), np.float32)
    for r in range(NCORES):
        pb, s, sb = r // 4, r % 4, r
        o = R[r]
        yT = np.asarray(o["yT"])
        y_s[sb] = yT[:, 0:16].T
        y_p[pb, s * NOWN:(s + 1) * NOWN] = yT[:, 16:].T
        pT = np.asarray(o["poolT"])
        for i in range(2):
            pool_s[i, sb] = pT[i][:, 1:16].T
            if s == 3:
                pool_p[i, pb] = pT[i][:, 17:32].T
        k_s[sb] = np.asarray(o["ksT"]).transpose(2, 0, 1)
        v_s[sb] = np.asarray(o["vs"]).reshape(128, 4, 64)
        if s == 3:
            k_p[pb] = np.asarray(o["knT"])[:, :, 16:144].transpose(2, 0, 1)
            v_p[pb] = np.asarray(o["vn"])[16:144].reshape(128, 4, 64)
    return (y_p, y_s, pool_p, k_p, v_p, pool_s, k_s, v_s)


def kernel(**inputs):
    inp = {k: np.asarray(v) for k, v in inputs.items()}
    if "nc" not in _CACHE:
        _CACHE["nc"] = build_program()
    nc = _CACHE["nc"]
    in_maps = prep_inputs(inp)
    res = run_bass_kernel_spmd(nc, in_maps, core_ids=list(range(NCORES)))
    return assemble(res.results, inp)
```

```python
import numpy as np
from contextlib import ExitStack
import concourse.bass as bass
import concourse.mybir as mybir
from concourse.bass_utils import run_bass_kernel_spmd

F32 = mybir.dt.float32
BF16 = mybir.dt.bfloat16
AF = mybir.ActivationFunctionType
ALU = mybir.AluOpType

NCORES = 8
D = 1024
KC = 8
T = 2224
C_OWN = 176
NOWN = 2048
SEGB = [0, 16, 176, 688, 1200, 1712, 2224]
BLK_A = [(0, 176), (176, 688), (688, 1200), (1200, 1712), (1712, 2224)]
BLK_B = [(0, 16), (176, 688), (688, 1200), (1200, 1712), (1712, 2224)]
NJ = 22
GROUPS = [(0, 4), (4, 8), (8, 12), (12, 16), (16, 20), (20, 22)]
NR = 4
PTL = 2256
EPS = 1e-6
ROT = 2000
NV_TILES = 19

V_BADA = 0
V_GMIX = 192
V_GFFN = 224
V_PSC = 256
V_GKV = 272
V_BKV = 280
V_GQ = 296
V_GK = 298
V_SINK = 299
V_FLAG = 331
V_INVC = 332
V_EPS = 460
NVEC = 461


def segkeys(name, c0, c1, *extra):
    ks = []
    for i in range(len(SEGB) - 1):
        if c0 < SEGB[i + 1] and c1 > SEGB[i]:
            ks.append((name,) + tuple(extra) + (i,))
    return ks


def colsegs(c0, c1):
    out = []
    if c0 < 16:
        out.append((c0, min(c1, 16), 1))
    if c1 > 16:
        out.append((max(c0, 16), c1, 0))
    return out


class Sched:
    def __init__(self):
        self.ops = []

    enabled = True

    def add(self, eng, fn, r=(), w=(), dma=None):
        if not self.enabled:
            return -1
        self.ops.append((eng, fn, tuple(r), tuple(w), dma))
        return len(self.ops) - 1

    def analyze(self):
        ops = self.ops
        last_w = {}
        readers = {}
        need_all = []
        signal = [False] * len(ops)
        for k, (eng, fn, r, w, dma) in enumerate(ops):
            raw = set()
            oth = set()
            for key in r:
                if key in last_w:
                    raw.add(last_w[key])
                if key[0] == "ps":
                    for idx in readers.get(key, {}).values():
                        oth.add(idx)
            for key in w:
                if key in last_w:
                    oth.add(last_w[key])
                for idx in readers.get(key, {}).values():
                    oth.add(idx)
            for key in r:
                readers.setdefault(key, {})[eng if dma is None else ("dma", k)] = k
            for key in w:
                last_w[key] = k
                readers[key] = {}
            need_dma = set()
            need_eng = {}
            for d in raw | oth:
                if d == k:
                    continue
                de, ddma = ops[d][0], ops[d][4]
                if ddma is not None:
                    need_dma.add(d)
                elif de != eng or dma is not None:
                    need_eng[de] = max(need_eng.get(de, -1), d)
                elif eng in ("act", "dve", "pool"):
                    need_eng[de] = max(need_eng.get(de, -1), d)
            need = sorted(need_dma | set(need_eng.values()))
            for d in need:
                signal[d] = True
            need_all.append(need)
        cnt = {}
        dcnt = {}
        event = {}
        for k, (eng, fn, r, w, dma) in enumerate(ops):
            if dma is not None:
                dcnt[dma] = dcnt.get(dma, 0) + 16
                event[k] = (("dma", dma), dcnt[dma])
            elif signal[k]:
                c = cnt.get(eng, 0)
                cnt[eng] = c + 1
                event[k] = ((eng, c // ROT), c % ROT + 1)
        self.need_all = need_all
        self.event = event
        self.final_dma = dict(dcnt)
        semkeys = set(sk for sk, _ in event.values())
        return sorted(semkeys, key=str)


def build_program():
    nc = bass.Bass("TRN2", target_bir_lowering=False)
    dt = nc.dram_tensor
    xT_d = dt("xT", [D, T], F32, kind="ExternalInput").ap()
    cT_d = dt("cT", [128, 16], F32, kind="ExternalInput").ap()
    vecs_d = dt("vecs", [128, NVEC], F32, kind="ExternalInput").ap()
    spT_d = dt("spT", [128, 256], F32, kind="ExternalInput").ap()
    ckT_d = dt("ckT", [128, 512], F32, kind="ExternalInput").ap()
    cv_d = dt("cv", [128, 384], F32, kind="ExternalInput").ap()
    cvraw_d = dt("cvraw", [128, 256], F32, kind="ExternalInput").ap()
    ws_d = dt("wstream", [N_SLOTS, 128, 2048], F32, kind="ExternalInput").ap()
    yT_d = dt("yT", [D, 16 + NOWN], F32, kind="ExternalOutput").ap()
    poolT_d = dt("poolT", [2, D, 32], F32, kind="ExternalOutput").ap()
    knT_d = dt("knT", [4, 64, 144], F32, kind="ExternalOutput").ap()
    vn_d = dt("vn", [144, 256], F32, kind="ExternalOutput").ap()
    ksT_d = dt("ksT", [4, 64, 128], F32, kind="ExternalOutput").ap()
    vs_d = dt("vs", [128, 256], F32, kind="ExternalOutput").ap()

    S = Sched()
    es = ExitStack()
    with es:
        off = [0]
        NW = 53200
        arena = es.enter_context(nc.sbuf_tensor("arena", [128, NW], F32))

        def carve(nbytes):
            n = (nbytes + 3) // 4
            a = arena[:, off[0]:off[0] + n]
            off[0] += n
            return a

        xT = carve(KC * T * 4).rearrange("p (c t) -> p c t", c=KC)
        rstd = carve(T * 4)
        ring = carve(NR * 2048 * 2).bitcast(BF16).rearrange("p (s n) -> p s n", s=NR)
        r1raw = carve(KC * T * 2)
        R1 = r1raw.bitcast(BF16).rearrange("p (c t) -> p c t", c=KC)
        PA = r1raw[:, 0:PTL]
        PB = r1raw[:, PTL:2 * PTL]
        PC = r1raw[:, 2 * PTL:3 * PTL]
        X4 = carve(4 * T * 2).bitcast(BF16).rearrange("p (c t) -> p c t", c=4)
        kv_off = off[0]
        kT = carve(4 * T * 2).bitcast(BF16).rearrange("p (c t) -> p c t", c=4)
        Vt = carve(NV_TILES * 384 * 2).bitcast(BF16).rearrange("p (j n) -> p j n", j=NV_TILES)
        assert off[0] - kv_off >= 3 * PTL
        PSETS = [(PA, PB, PC), tuple(arena[:, kv_off + k * PTL:kv_off + (k + 1) * PTL] for k in range(3))]
        tmpf = carve(4 * 512 * 4).rearrange("p (i n) -> p i n", i=4)
        sqb = carve(2 * 512 * 2).bitcast(BF16).rearrange("p (i n) -> p i n", i=2)
        PTb = carve(2 * 512 * 2).bitcast(BF16).rearrange("p (i n) -> p i n", i=2)
        rcb = carve(3 * 256 * 4).rearrange("p (i n) -> p i n", i=3)
        sinkexp = carve(256 * 4)
        vecs = carve(NVEC * 4)
        cTs = carve(16 * 4)
        cact = carve(16 * 2).bitcast(BF16).rearrange("p (c v) -> p c v", v=2)
        ada = carve(5 * 96 * 4).rearrange("p (l o v) -> p l o v", l=5, v=2)
        der = carve(4 * 64 * 4 + 64)
        ones = carve(128 * 2).bitcast(BF16)
        bones = carve(128 * 2).bitcast(BF16)
        zer = carve(64 * 4)
        spT = carve(256 * 4).rearrange("p (i c r) -> p i c r", i=2, c=8)
        poolout = carve(2 * 8 * 32 * 4).rearrange("p (i c r) -> p i c r", i=2, c=8)
        knout = carve(4 * 144 * 4).rearrange("p (k c) -> p k c", k=4)
        vnout = carve(2 * 256 * 4).rearrange("p (i n) -> p i n", i=2)
        ckT = carve(512 * 2).bitcast(BF16).rearrange("p (k n) -> p k n", k=4)
        print('arena words used', off[0], 'of', NW)
        assert off[0] <= NW, off[0]

        ps = [es.enter_context(nc.psum_tensor("ps%d" % i, [128, 512], F32)) for i in range(8)]
        bank_ctr = {"A": 0, "B": 0, "O": 0, "O3": 0, "S": 0, "OW": 0, "A4": 0}
        bank_map = {"A": [0, 1], "B": [2, 3], "O": [4, 5], "O3": [4, 5, 6, 7], "S": [6, 7], "OW": [4, 5, 0, 1, 2, 3], "A4": [0, 1, 2, 3]}

        def bank(role):
            lst = bank_map[role]
            b = lst[bank_ctr[role] % len(lst)]
            bank_ctr[role] += 1
            return b

        def dG1(i):
            return der[:, i * 64:i * 64 + 16].rearrange("p (c v) -> p c v", v=2)

        def dG2(i):
            return der[:, i * 64 + 16:i * 64 + 32].rearrange("p (c v) -> p c v", v=2)

        def dGP(i):
            return der[:, i * 64 + 32:i * 64 + 48].rearrange("p (c v) -> p c v", v=2)

        def dG1h(i):
            return der[:, i * 64 + 48:i * 64 + 56]

        def dS1h(i):
            return der[:, i * 64 + 56:i * 64 + 64]

        dGkv = der[:, 256:272].rearrange("p (c v) -> p c v", v=2)
        eps_col = vecs[:, V_EPS:V_EPS + 1]
        flag_col = vecs[:, V_FLAG:V_FLAG + 1]

        def adav(i, q):
            return ada[:, i, q * 8:(q + 1) * 8, :]

        slot_i = [0]

        SLOT_DESCS.clear()

        def next_slot(desc):
            k = slot_i[0]
            slot_i[0] += 1
            SLOT_DESCS.append(desc)
            s = k % NR
            S.add("pool", lambda e, k=k, s=s: e.dma_start(out=ring[:, s, :], in_=ws_d[k]),
                  w=[("ring", s)], dma="ring%d" % s)
            return s

        xT_dv = xT_d.rearrange("(c p) t -> p c t", p=128)
        for kc in range(KC):
            S.add("sp", lambda e, kc=kc: e.dma_start(out=xT[:, kc, :], in_=xT_dv[:, kc, :]),
                  w=segkeys("x", 0, T, kc), dma="ldx%d" % kc)
        S.add("sp", lambda e: e.dma_start(out=vecs, in_=vecs_d), w=[("vecs",)], dma="ldvec")
        S.add("sp", lambda e: e.dma_start(out=cTs, in_=cT_d), w=[("cTs",)], dma="ldc")
        S.add("sp", lambda e: e.dma_start(out=spT.rearrange("p i c r -> p (i c r)"), in_=spT_d),
              w=[("spT",)], dma="ldsp")
        S.add("pool", lambda e: e.dma_start(out=ckT.rearrange("p k n -> p (k n)"), in_=ckT_d),
              w=[("ckT",)], dma="ldck")
        ckT_dv = ckT_d.rearrange("p (k n) -> p k n", k=4)
        S.add("sp", lambda e: e.dma_start(out=ksT_d.rearrange("k d c -> d k c")[:, :, 0:112],
                                          in_=ckT_dv[0:64, :, 16:128]), dma="o_ks0")
        S.add("sp", lambda e: e.dma_start(out=vs_d[0:112, :], in_=cvraw_d[16:128, :]), dma="o_vs0")

        import os
        kpro = int(os.environ.get("KPRO", "9"))
        S.enabled = kpro >= 1
        S.add("dve", lambda e: e.memset(ones, 1.0), w=[("ones",)])
        S.add("dve", lambda e: e.memset(bones, 0.0), w=[("bones",)])
        S.add("dve", lambda e: e.memset(bones[0:64, 0:64], 1.0), w=[("bones",)])
        S.add("dve", lambda e: e.memset(bones[64:128, 64:128], 1.0), w=[("bones",)])
        S.add("dve", lambda e: e.memset(zer, 0.0), w=[("zer",)])
        S.add("act", lambda e: e.activation(out=cact.rearrange("p c v -> p (c v)"), in_=cTs, func=AF.Silu),
              r=[("cTs",)], w=[("cact",)])

        bz = 7
        DERc = [None]

        def DERL(i):
            return [("der", i), ("ada", i), ("vecs",)]

        def ada_job(li, obp):
            def run():
                s = next_slot(("ada", li, obp))
                for o in range(2):
                    ob = 2 * obp + o
                    colbase = li * 96 + ob * 2
                    for kc in range(KC):
                        S.add("pe", lambda e, s=s, o=o, kc=kc, cb=colbase: e.matmul(
                            ps[bz][:, cb:cb + 2], lhsT=ring[:, s, (o * 8 + kc) * 128:(o * 8 + kc + 1) * 128],
                            rhs=cact[:, kc, :], start=(kc == 0), stop=(kc == 7)),
                            r=[("ring", s), ("cact",)], w=[("ps", bz)])
            return run

        def ada_finish(li, part=None):
            def run():
                nob = 48 if li < 4 else 16
                boff = V_BADA + li * 48 if li < 4 else V_BKV
                o0, o1 = (0, nob) if part is None else ((0, 24) if part == 0 else (24, 48))
                for v in range(2):
                    S.add("dve", lambda e, v=v: e.tensor_tensor(
                        out=ada[:, li, o0:o1, v],
                        in0=ps[bz][:, li * 96 + 2 * o0:li * 96 + 2 * o1].rearrange("p (o v) -> p o v", v=2)[:, :, v],
                        in1=vecs[:, boff + o0:boff + o1], op=ALU.add),
                        r=[("ps", bz), ("vecs",)], w=[("ada", li)])
                if li == 4:
                    gkv = vecs[:, V_GKV:V_GKV + 8]
                    for v in range(2):
                        S.add("dve", lambda e, v=v: e.scalar_tensor_tensor(
                            out=dGkv[:, :, v], in0=ada[:, 4, 8:16, v], scalar=1.0, in1=gkv, op0=ALU.add, op1=ALU.mult),
                            r=[("ada", 4), ("vecs",)], w=[("der", 4)])
                    return
                i = li
                gm = vecs[:, V_GMIX + i * 8:V_GMIX + i * 8 + 8]
                gf = vecs[:, V_GFFN + i * 8:V_GFFN + i * 8 + 8]
                for v in range(2):
                    if part in (None, 0):
                        S.add("dve", lambda e, v=v: e.scalar_tensor_tensor(
                            out=dG1(i)[:, :, v], in0=adav(i, 1)[:, :, v], scalar=1.0, in1=gm, op0=ALU.add, op1=ALU.mult),
                            r=[("ada", i), ("vecs",)], w=[("der", i)])
                    if part in (None, 1):
                        S.add("dve", lambda e, v=v: e.scalar_tensor_tensor(
                            out=dG2(i)[:, :, v], in0=adav(i, 4)[:, :, v], scalar=1.0, in1=gf, op0=ALU.add, op1=ALU.mult),
                            r=[("ada", i), ("vecs",)], w=[("der", i)])
                    if i < 2 and part in (None, 0):
                        psc = vecs[:, V_PSC + i * 8:V_PSC + i * 8 + 8]
                        S.add("dve", lambda e, v=v, psc=psc: e.tensor_tensor(
                            out=dGP(i)[:, :, v], in0=adav(i, 2)[:, :, v], in1=psc, op=ALU.mult),
                            r=[("ada", i), ("vecs",)], w=[("der", i)])
                if i < 2 and part in (None, 0):
                    S.add("dve", lambda e: e.tensor_scalar(
                        out=dG1h(i), in0=dG1(i)[:, :, 0], scalar1=flag_col, scalar2=None, op0=ALU.mult),
                        r=[("der", i), ("vecs",)], w=[("der", i)])
                    S.add("dve", lambda e: e.tensor_scalar(
                        out=dS1h(i), in0=adav(i, 0)[:, :, 0], scalar1=flag_col, scalar2=None, op0=ALU.mult),
                        r=[("ada", i), ("vecs",)], w=[("der", i)])
            return run

        pending = []

        def queue_ada(li):
            nob = 48 if li < 4 else 16
            for obp in range(nob // 2):
                pending.append(ada_job(li, obp))
            pending.append(ada_finish(li))

        def drain(n):
            for _ in range(min(n, len(pending))):
                pending.pop(0)()


        sq_ctr = [0]

        def norm_stats(blocks, alt=False):
            for (c0, c1) in blocks:
                N = c1 - c0
                b = bank("S")
                tb = 2 + (sq_ctr[0] % 2)
                sq_ctr[0] += 1
                for kc in range(KC):
                    sb = kc % 4
                    sbuf = sqb[:, sb, 0:N] if sb < 2 else PTb[:, sb - 2, 0:N]
                    skey = ("sqb", sb) if sb < 2 else ("PT", sb - 2)
                    if alt and kc % 2 == 1:
                        S.add("dve", lambda e, kc=kc, sbuf=sbuf, c0=c0, c1=c1: e.tensor_tensor(
                            out=sbuf, in0=xT[:, kc, c0:c1], in1=xT[:, kc, c0:c1], op=ALU.mult),
                            r=segkeys("x", c0, c1, kc), w=[skey])
                    else:
                        S.add("act", lambda e, kc=kc, sbuf=sbuf, c0=c0, c1=c1: e.activation(
                            out=sbuf, in_=xT[:, kc, c0:c1], func=AF.Square),
                            r=segkeys("x", c0, c1, kc), w=[skey])
                    S.add("pe", lambda e, kc=kc, sbuf=sbuf, N=N, b=b: e.matmul(
                        ps[b][:, 0:N], lhsT=ones, rhs=sbuf, start=(kc == 0), stop=(kc == 7)),
                        r=[skey, ("ones",)], w=[("ps", b)])
                S.add("act", lambda e, N=N, b=b, tb=tb: e.activation(
                    out=tmpf[:, tb, 0:N], in_=ps[b][:, 0:N], func=AF.Ln, bias=eps_col, scale=1.0 / D),
                    r=[("ps", b), ("vecs",)], w=[("tmp", tb)])
                S.add("act", lambda e, N=N, c0=c0, c1=c1, tb=tb: e.activation(
                    out=rstd[:, c0:c1], in_=tmpf[:, tb, 0:N], func=AF.Exp, scale=-0.5),
                    r=[("tmp", tb)], w=segkeys("rstd", c0, c1))

        ucnt = [0]

        def norm_apply(blocks, Gv, Sv, out):
            for (c0, c1) in blocks:
                for kc in range(KC):
                    for (a, b_, v) in colsegs(c0, c1):
                        n = b_ - a
                        ti = ucnt[0] % 2
                        ucnt[0] += 1
                        S.add("dve", lambda e, kc=kc, a=a, b_=b_, n=n, ti=ti: e.tensor_tensor(
                            out=tmpf[:, ti, 0:n], in0=xT[:, kc, a:b_], in1=rstd[:, a:b_], op=ALU.mult),
                            r=segkeys("x", a, b_, kc) + segkeys("rstd", a, b_), w=[("tmp", ti)])
                        S.add("act", lambda e, kc=kc, a=a, b_=b_, n=n, ti=ti, v=v: e.activation(
                            out=out[:, kc, a:b_], in_=tmpf[:, ti, 0:n], func=AF.Identity,
                            bias=Sv[:, kc, v:v + 1], scale=Gv[:, kc, v:v + 1]),
                            r=[("tmp", ti)] + DERc[0], w=segkeys("R1", a, b_, kc))

        def norm_full(blocks, Gv, Sv, out):
            for bi, blk in enumerate(blocks):
                norm_stats([blk], alt=True)
                if bi >= 1:
                    norm_apply([blocks[bi - 1]], Gv, Sv, out)
            norm_apply([blocks[-1]], Gv, Sv, out)

        hn_ctr = [0]

        def head_norm(b, N, gcol, dest, dest_keys, caps=(), sbank=None):
            k = hn_ctr[0] % 2
            hn_ctr[0] += 1
            sb, tb, bs = k, 2 + k, (bank("S") if sbank is None else sbank)
            S.add("act", lambda e: e.activation(out=sqb[:, sb, 0:N], in_=ps[b][:, 0:N], func=AF.Square),
                  r=[("ps", b)], w=[("sqb", sb)])
            S.add("pe", lambda e: e.matmul(ps[bs][:, 0:N], lhsT=bones, rhs=sqb[:, sb, 0:N], start=True, stop=True),
                  r=[("sqb", sb), ("bones",)], w=[("ps", bs)])
            S.add("act", lambda e: e.activation(
                out=tmpf[:, tb, 0:N], in_=ps[bs][:, 0:N], func=AF.Ln, bias=eps_col, scale=1.0 / 64),
                r=[("ps", bs), ("vecs",)], w=[("tmp", tb)])
            S.add("act", lambda e: e.activation(
                out=tmpf[:, tb, 0:N], in_=tmpf[:, tb, 0:N], func=AF.Exp, scale=-0.5),
                r=[("tmp", tb)], w=[("tmp", tb)])
            S.add("dve", lambda e: e.scalar_tensor_tensor(
                out=dest, in0=ps[b][:, 0:N], scalar=gcol, in1=tmpf[:, tb, 0:N], op0=ALU.mult, op1=ALU.mult),
                r=[("ps", b), ("tmp", tb), ("vecs",)], w=dest_keys)
            for (pa, pb_, cdst, ckeys) in caps:
                S.add("dve", lambda e, pa=pa, pb_=pb_, cdst=cdst: e.scalar_tensor_tensor(
                    out=cdst, in0=ps[b][0:64, pa:pb_], scalar=gcol[0:64, :], in1=tmpf[0:64, tb, pa:pb_],
                    op0=ALU.mult, op1=ALU.mult),
                    r=[("ps", b), ("tmp", tb), ("vecs",)], w=ckeys)

        def x_update(b, blk, m, gvec):
            c0, c1 = blk
            for (a, b_, v) in colsegs(c0, c1):
                S.add("dve", lambda e, b=b, m=m, a=a, b_=b_, v=v, c0=c0: e.scalar_tensor_tensor(
                    out=xT[:, m, a:b_], in0=ps[b][:, a - c0:b_ - c0], scalar=gvec[:, m, v:v + 1],
                    in1=xT[:, m, a:b_], op0=ALU.mult, op1=ALU.add),
                    r=[("ps", b)] + DERc[0] + segkeys("x", a, b_, m), w=segkeys("x", a, b_, m))

        def ffn(i, blocks):
            norm_full(blocks, dG2(i), adav(i, 3), R1)
            g2 = adav(i, 5)
            sgc = [0]
            for (j0, j1) in GROUPS:
                for jj, j in enumerate(range(j0, j1)):
                    s = next_slot(("ffn_in", i, j))
                    for (c0, c1) in blocks:
                        N = c1 - c0
                        bg = bank("A")
                        bu = bank("B")
                        for kc in range(KC):
                            S.add("pe", lambda e, s=s, kc=kc, c0=c0, c1=c1, N=N, bg=bg: e.matmul(
                                ps[bg][:, 0:N], lhsT=ring[:, s, kc * 128:(kc + 1) * 128], rhs=R1[:, kc, c0:c1],
                                start=(kc == 0), stop=(kc == 7)),
                                r=[("ring", s)] + segkeys("R1", c0, c1, kc), w=[("ps", bg)])
                        for kc in range(KC):
                            S.add("pe", lambda e, s=s, kc=kc, c0=c0, c1=c1, N=N, bu=bu: e.matmul(
                                ps[bu][:, 0:N], lhsT=ring[:, s, (8 + kc) * 128:(9 + kc) * 128], rhs=R1[:, kc, c0:c1],
                                start=(kc == 0), stop=(kc == 7)),
                                r=[("ring", s)] + segkeys("R1", c0, c1, kc), w=[("ps", bu)])
                        ti = sgc[0] % 2
                        sgc[0] += 1
                        S.add("act", lambda e, bg=bg, N=N, ti=ti: e.activation(
                            out=tmpf[:, ti, 0:N], in_=ps[bg][:, 0:N], func=AF.Silu),
                            r=[("ps", bg)], w=[("tmp", ti)])
                        S.add("dve", lambda e, bu=bu, N=N, ti=ti, jj=jj, c0=c0, c1=c1: e.tensor_tensor(
                            out=X4[:, jj, c0:c1], in0=tmpf[:, ti, 0:N], in1=ps[bu][:, 0:N], op=ALU.mult),
                            r=[("ps", bu), ("tmp", ti)], w=segkeys("X4", c0, c1, jj))
                    drain(2)
                nj = j1 - j0
                oslots = [next_slot(("ffn_out", i, j0 + 2 * q_)) for q_ in range(nj // 2)]
                for (c0, c1) in blocks:
                    N = c1 - c0
                    for m in range(KC):
                        bo = bank("OW")
                        for jj in range(nj):
                            s = oslots[jj // 2]
                            o = jj % 2
                            S.add("pe", lambda e, s=s, o=o, m=m, jj=jj, c0=c0, c1=c1, N=N, bo=bo, nj=nj: e.matmul(
                                ps[bo][:, 0:N], lhsT=ring[:, s, o * 1024 + m * 128:o * 1024 + (m + 1) * 128],
                                rhs=X4[:, jj, c0:c1], start=(jj == 0), stop=(jj == nj - 1)),
                                r=[("ring", s)] + segkeys("X4", c0, c1, jj), w=[("ps", bo)])
                        x_update(bo, (c0, c1), m, g2)
            drain(len(pending))

        def ptc(t):
            return 16 + t if t < 16 else 32 + t

        def layer_a(i):
            DERc[0] = DERL(i)
            norm_stats(BLK_A)
            if i == 0:
                for obp in range(12):
                    pending.append(ada_job(0, obp))
                pending.append(ada_finish(0, part=0))
                for obp in range(12, 24):
                    pending.append(ada_job(0, obp))
                drain(len(pending))
                pending.append(ada_finish(0, part=1))
            wps = next_slot(("pool", i))
            G1, S1, G1h, S1h, GP = dG1(i), adav(i, 0), dG1h(i), dS1h(i), dGP(i)

            def stage1(kc):
                k = kc % 2
                PA_, PB_, PC_ = PSETS[k]
                ka, kc_ = ("pa", k), ("pc", k)
                S.add("dve", lambda e: e.tensor_copy(out=PA_[:, 0:16], in_=spT[:, i, kc, :]), r=[("spT",)], w=[ka])
                S.add("dve", lambda e: e.memset(PA_[:, 32:48], 0.0), w=[ka])
                S.add("dve", lambda e: e.tensor_tensor(
                    out=PC_[:, 16:32], in0=xT[:, kc, 0:16], in1=rstd[:, 0:16], op=ALU.mult),
                    r=segkeys("x", 0, 16, kc) + segkeys("rstd", 0, 16), w=[kc_])
                S.add("dve", lambda e: e.tensor_tensor(
                    out=PC_[:, 48:PTL], in0=xT[:, kc, 16:T], in1=rstd[:, 16:T], op=ALU.mult),
                    r=segkeys("x", 16, T, kc) + segkeys("rstd", 16, T), w=[kc_])
                S.add("act", lambda e: e.activation(
                    out=PA_[:, 16:32], in_=PC_[:, 16:32], func=AF.Identity,
                    bias=S1[:, kc, 1:2], scale=G1[:, kc, 1:2]), r=[kc_] + DERc[0], w=[ka])
                S.add("act", lambda e: e.activation(
                    out=PA_[:, 48:208], in_=PC_[:, 48:208], func=AF.Identity,
                    bias=S1h[:, kc:kc + 1], scale=G1h[:, kc:kc + 1]), r=[kc_] + DERc[0], w=[ka])
                S.add("act", lambda e: e.activation(
                    out=PA_[:, 208:PTL], in_=PC_[:, 208:PTL], func=AF.Identity,
                    bias=S1[:, kc, 0:1], scale=G1[:, kc, 0:1]), r=[kc_] + DERc[0], w=[ka])
                S.add("act", lambda e: e.activation(
                    out=poolout[:, i, kc, 0:16], in_=PA_[:, 16:32], func=AF.Identity), r=[ka], w=[("poolout",)])
                S.add("act", lambda e: e.activation(
                    out=poolout[:, i, kc, 16:32], in_=PA_[:, PTL - 16:PTL], func=AF.Identity), r=[ka], w=[("poolout",)])

            def stage2(kc):
                k = kc % 2
                g = kc // 2
                PA_, PB_, PC_ = PSETS[k]
                ka = ("pa", k)
                dslot = 2 * (g % 2) + (kc % 2)
                seq = [(PA_, ("pa", k)), (PB_, ("pb", k)), (PC_, ("pc", k)), (PB_, ("pb", k)), (PC_, ("pc", k))]
                cur, curk = seq[0]
                for L in range(1, g + 2):
                    st = 2 ** (L - 1)
                    lo = 2 ** L - 1
                    dst, dstk = seq[L]
                    S.add("dve", lambda e, cur=cur, dst=dst, st=st, lo=lo: e.tensor_tensor(
                        out=dst[:, lo:PTL], in0=cur[:, lo:PTL], in1=cur[:, lo - st:PTL - st], op=ALU.add),
                        r=[curk], w=[dstk])
                    cur, curk = dst, dstk
                inv = 1.0 / (2 ** (g + 1))
                S.add("dve", lambda e: e.scalar_tensor_tensor(
                    out=X4[:, dslot, 0:16], in0=cur[:, 16:32], scalar=inv, in1=PA_[:, 16:32],
                    op0=ALU.mult, op1=ALU.subtract), r=[curk, ka], w=segkeys("X4", 0, 16, dslot))
                S.add("dve", lambda e: e.scalar_tensor_tensor(
                    out=X4[:, dslot, 16:T], in0=cur[:, 48:PTL], scalar=inv, in1=PA_[:, 48:PTL],
                    op0=ALU.mult, op1=ALU.subtract), r=[curk, ka], w=segkeys("X4", 16, T, dslot))
                S.add("dve", lambda e: e.tensor_tensor(
                    out=tmpf[:, 3, 0:16], in0=cur[:, 208:224], in1=vecs[:, V_INVC + kc * 16:V_INVC + kc * 16 + 16],
                    op=ALU.mult), r=[curk, ("vecs",)], w=[("tmp", 3)])
                S.add("dve", lambda e: e.tensor_tensor(
                    out=X4[:, dslot, 176:192], in0=tmpf[:, 3, 0:16], in1=PA_[:, 208:224], op=ALU.subtract),
                    r=[("tmp", 3), ka], w=segkeys("X4", 176, 192, dslot))

            def group_out(g):
                for ec in range(2):
                    m = 2 * g + ec
                    for (c0, c1) in BLK_A:
                        N = c1 - c0
                        bo = bank("OW")
                        for kk in range(2):
                            dslot = 2 * (g % 2) + kk
                            wo = ((g * 2 + ec) * 2 + kk) * 128
                            S.add("pe", lambda e, wo=wo, dslot=dslot, kk=kk, c0=c0, c1=c1, N=N, bo=bo: e.matmul(
                                ps[bo][:, 0:N], lhsT=ring[:, wps, wo:wo + 128], rhs=X4[:, dslot, c0:c1],
                                start=(kk == 0), stop=(kk == 1)),
                                r=[("ring", wps)] + segkeys("X4", c0, c1, dslot), w=[("ps", bo)])
                        x_update(bo, (c0, c1), m, GP)

            stage1(0)
            for kc in range(KC):
                if kc + 1 < KC:
                    stage1(kc + 1)
                stage2(kc)
                if kc % 2 == 1:
                    if i == 0 and kc == 3:
                        drain(1)
                    group_out(kc // 2)
            drain(len(pending))
            if i == 0:
                queue_ada(1)
                queue_ada(4)
            else:
                queue_ada(2)
            ffn(i, BLK_A)

        def kv_phase():
            DERc[0] = DERL(4)
            alias = [("pa", 1), ("pb", 1), ("pc", 1)]
            S.add("dve", lambda e: e.memset(Vt[:, 0:17, 64:128], 1.0), w=[("V", j) for j in range(17)] + alias)
            S.add("dve", lambda e: e.memset(Vt[:, 0:17, 256:320], 1.0), w=[("V", j) for j in range(17)] + alias)
            S.add("dve", lambda e: e.memset(Vt[0:32, 17, :], 0.0), w=[("V", 17)] + alias)
            S.add("dve", lambda e: e.memset(Vt[0:16, 17, 64:128], 1.0), w=[("V", 17)] + alias)
            S.add("dve", lambda e: e.memset(Vt[0:16, 17, 256:320], 1.0), w=[("V", 17)] + alias)
            S.add("pool", lambda e: e.dma_start(out=Vt[:, 18, :], in_=cv_d), w=[("V", 18)] + alias, dma="ldcv")
            norm_full(BLK_A, dGkv, ada[:, 4, 0:8, :], R1)
            gk = vecs[:, V_GK:V_GK + 1]
            ks = [next_slot(("wk", 0)), next_slot(("wk", 1))]
            vsl = next_slot(("wv",))
            def v_job(j, t0, nt, idx):
                bz_ = 4 + (idx % 2)
                mt = max(nt, 32)
                for kc in range(KC):
                    S.add("pe", lambda e, kc=kc: e.matmul(
                        ps[bz_][0:mt, 0:256], lhsT=R1[:, kc, t0:t0 + mt], rhs=ring[:, vsl, kc * 256:(kc + 1) * 256],
                        start=(kc == 0), stop=(kc == 7)),
                        r=[("ring", vsl)] + segkeys("R1", t0, t0 + mt, kc), w=[("ps", bz_)])
                pv4 = ps[bz_][0:nt, 0:256].rearrange("p (k d) -> p k d", k=4)
                S.add("dve", lambda e: e.tensor_copy(
                    out=Vt[0:nt, j, 0:192].rearrange("p (k d) -> p k d", d=64)[:, 0:3:2, :], in_=pv4[:, 0:2, :]),
                    r=[("ps", bz_)], w=[("V", j)])
                S.add("dve", lambda e: e.tensor_copy(
                    out=Vt[0:nt, j, 192:384].rearrange("p (k d) -> p k d", d=64)[:, 0:3:2, :], in_=pv4[:, 2:4, :]),
                    r=[("ps", bz_)], w=[("V", j)])
                if j == 16:
                    S.add("dve", lambda e: e.tensor_copy(out=vnout[:, 0, :], in_=ps[bz_][:, 0:256]),
                          r=[("ps", bz_)], w=[("vnout",)])
                if j == 17:
                    S.add("dve", lambda e: e.tensor_copy(out=vnout[0:16, 1, :], in_=ps[bz_][0:16, 0:256]),
                          r=[("ps", bz_)], w=[("vnout",)])
                if j == 0:
                    S.add("dve", lambda e: e.tensor_scalar(
                        out=Vt[:, 0, :], in0=Vt[:, 0, :], scalar1=flag_col, scalar2=None, op0=ALU.mult),
                        r=[("V", 0), ("vecs",)], w=[("V", 0)])

            vjobs = [(17, 0, 16, 0)] + [(j, 48 + 128 * j, 128, j + 1) for j in range(17)]
            khn = []
            for kvh in range(4):
                s = ks[kvh // 2]
                base = (kvh % 2) * 1024
                for (c0, c1) in BLK_A:
                    N = c1 - c0
                    b = bank("A4")
                    for kc in range(KC):
                        S.add("pe", lambda e, s=s, base=base, kc=kc, c0=c0, c1=c1, N=N, b=b: e.matmul(
                            ps[b][:, 0:N], lhsT=ring[:, s, base + kc * 128:base + (kc + 1) * 128],
                            rhs=R1[:, kc, c0:c1], start=(kc == 0), stop=(kc == 7)),
                            r=[("ring", s)] + segkeys("R1", c0, c1, kc), w=[("ps", b)])
                    caps = []
                    if c0 == 0:
                        caps.append((0, 16, knout[0:64, kvh, 0:16], [("knout",)]))
                    if c1 == T:
                        caps.append((N - 128, N, knout[0:64, kvh, 16:144], [("knout",)]))
                    if khn:
                        khn.pop(0)()
                    khn.append(lambda b=b, N=N, kvh=kvh, c0=c0, c1=c1, caps=caps: head_norm(
                        b, N, gk, kT[:, kvh, c0:c1], segkeys("kT", c0, c1, kvh), caps))
                    if vjobs:
                        v_job(*vjobs.pop(0))
            while khn:
                khn.pop(0)()
            while vjobs:
                v_job(*vjobs.pop(0))
            S.add("sp", lambda e: e.dma_start(out=knT_d.rearrange("k d c -> d k c"), in_=knout[0:64, :, :]),
                  r=[("knout",)], dma="o_kn")
            S.add("sp", lambda e: e.dma_start(out=ksT_d.rearrange("k d c -> d k c")[:, :, 112:128],
                                              in_=knout[0:64, :, 0:16]), r=[("knout",)], dma="o_ks1")
            S.add("sp", lambda e: e.dma_start(out=vn_d[16:144, :], in_=vnout[:, 0, :]), r=[("vnout",)], dma="o_vn0")
            S.add("sp", lambda e: e.dma_start(out=vn_d[0:16, :], in_=vnout[0:16, 1, :]), r=[("vnout",)], dma="o_vn1")
            S.add("sp", lambda e: e.dma_start(out=vs_d[112:128, :], in_=vnout[0:16, 1, :]), r=[("vnout",)], dma="o_vs1")
            S.add("sp", lambda e: e.dma_start(out=poolT_d.rearrange("i (c p) r -> p i c r", p=128), in_=poolout),
                  r=[("poolout",)], dma="o_pool")

        VC0 = [0, 64, 192, 256]

        def layer_b(i):
            jl = i - 2
            DERc[0] = DERL(i)
            if i == 2:
                queue_ada(3)
            if i == 2:
                norm_apply(BLK_B, dG1(i), adav(i, 0), R1)
            else:
                norm_full(BLK_B, dG1(i), adav(i, 0), R1)
            g1 = adav(i, 2)
            gq = vecs[:, V_GQ + jl:V_GQ + jl + 1]
            uc = [0]
            rq = rstd.bitcast(BF16).rearrange("p (c t) -> p c t", c=2)
            Qb = [X4, rq]

            def qkeys(qi, c0, c1, o):
                return segkeys("X4", c0, c1, o) if qi == 0 else segkeys("rstd", c0, c1)

            def qproj_jobs(kvh, inline):
                state = {}
                jobs = []

                def mk(o, c0, c1):
                    def run():
                        if "qs" not in state:
                            state["qs"] = next_slot(("wq", jl, kvh))
                        qs = state["qs"]
                        N = c1 - c0
                        b = 7 if inline else bank("A4")
                        for kc in range(KC):
                            S.add("pe", lambda e, kc=kc: e.matmul(
                                ps[b][:, 0:N], lhsT=ring[:, qs, (o * 8 + kc) * 128:(o * 8 + kc + 1) * 128],
                                rhs=R1[:, kc, c0:c1], start=(kc == 0), stop=(kc == 7)),
                                r=[("ring", qs)] + segkeys("R1", c0, c1, kc), w=[("ps", b)])
                        prev = state.pop("hn", None)
                        if prev is not None:
                            prev()
                        state["hn"] = lambda: head_norm(b, N, gq, rq[:, o, c0:c1], qkeys(1, c0, c1, o),
                                                        sbank=(6 if inline else None))
                    return run

                def flush():
                    prev = state.pop("hn", None)
                    if prev is not None:
                        prev()
                for o in range(2):
                    for (c0, c1) in BLK_B:
                        jobs.append(mk(o, c0, c1))
                jobs.append(flush)
                return jobs

            qdone = set()
            for kvh in range(4):
                if kvh not in qdone:
                    for jb in qproj_jobs(kvh, False):
                        jb()
                for blk in range(4):
                    par, o = blk // 2, blk % 2
                    h = kvh * 4 + 2 * o + par
                    S.add("act", lambda e, blk=blk, h=h: e.activation(
                        out=sinkexp[:, blk * 64:(blk + 1) * 64], in_=zer, func=AF.Exp,
                        bias=vecs[:, V_SINK + jl * 16 + h:V_SINK + jl * 16 + h + 1], scale=1.0),
                        r=[("zer",), ("vecs",)], w=[("sinkexp",)])
                nxtq = []
                Qc = rq
                qi = 1
                a0 = 2 * (kvh % 2)
                nb = 0 if kvh % 2 == 0 else 64
                db = 64 - nb
                vc = VC0[kvh]
                units = []
                units.append((0, 16,
                              (lambda hb, kvh=kvh: ckT[hb:hb + 64, kvh, 0:128], 18, [("ckT",)]),
                              (lambda hb, kvh=kvh: kT[hb:hb + 64, kvh, 0:32], 32, 0, 32, 17, segkeys("kT", 0, 32, kvh))))
                for c in range(32):
                    if c % 2 == 0:
                        jf, jh, hh = c // 2, c // 2 + 1, 0
                    else:
                        jf, jh, hh = (c + 1) // 2, (c - 1) // 2, 1
                    cf, ch = 48 + 128 * jf, 48 + 128 * jh
                    units.append((C_OWN + 64 * c, 64,
                                  (lambda hb, cf=cf, kvh=kvh: kT[hb:hb + 64, kvh, cf:cf + 128], jf,
                                   segkeys("kT", cf, cf + 128, kvh)),
                                  (lambda hb, ch=ch, kvh=kvh: kT[hb:hb + 64, kvh, ch:ch + 128], 128, 64 * hh, 64 * hh + 64,
                                   jh, segkeys("kT", ch, ch + 128, kvh))))

                def emit_front(u, ui):
                    q0, nq, full, part = u
                    bpar = [bank("A"), bank("B")]
                    pt = ui % 2
                    for fh in range(2):
                        src = full if fh == 0 else part
                        M = 128 if fh == 0 else part[1]
                        keys = full[2] if fh == 0 else part[5]
                        for o in range(2):
                            for par in range(2):
                                hb = par * 64
                                bb = bpar[par]
                                cbase = fh * 256 + o * nq
                                S.add("pe", lambda e, o=o, hb=hb, q0=q0, nq=nq, bb=bb, src=src, M=M, cbase=cbase, Qc=Qc: e.matmul(
                                    ps[bb][0:M, cbase:cbase + nq], lhsT=src[0](hb), rhs=Qc[hb:hb + 64, o, q0:q0 + nq],
                                    start=True, stop=True),
                                    r=keys + qkeys(qi, q0, q0 + nq, o), w=[("ps", bb)])
                    for par in range(2):
                        bb = bpar[par]
                        S.add("act", lambda e, nq=nq, bb=bb, pt=pt, par=par: e.activation(
                            out=PTb[:, pt, par * 256:(par + 1) * 256].rearrange("p (f x) -> p f x", f=2)[:, :, 0:2 * nq],
                            in_=ps[bb][:, 0:512].rearrange("p (f x) -> p f x", f=2)[:, :, 0:2 * nq],
                            func=AF.Exp, scale=0.125),
                            r=[("ps", bb)], w=[("PT", pt)])
                    return (u, pt)

                def emit_B(st):
                    u, pt = st
                    q0, nq, full, part = u
                    W = 4 * nq
                    bo = bank("O3")
                    r0, r1_ = part[2], part[3]
                    S.add("pe", lambda e, W=W, nq=nq, bo=bo, pt=pt, full=full, vc=vc: e.matmul(
                        ps[bo][:, 0:W].rearrange("p (a x) -> p a x", a=2), lhsT=Vt[:, full[1], vc:vc + 128],
                        rhs=PTb[:, pt, :].rearrange("p (a f x) -> p a f x", a=2, f=2)[:, :, 0, 0:2 * nq],
                        start=True, stop=False),
                        r=[("V", full[1]), ("PT", pt)], w=[("ps", bo)])
                    S.add("pe", lambda e, W=W, nq=nq, bo=bo, pt=pt, part=part, r0=r0, r1_=r1_, vc=vc: e.matmul(
                        ps[bo][:, 0:W].rearrange("p (a x) -> p a x", a=2), lhsT=Vt[r0:r1_, part[4], vc:vc + 128],
                        rhs=PTb[r0:r1_, pt, :].rearrange("p (a f x) -> p a f x", a=2, f=2)[:, :, 1, 0:2 * nq],
                        start=False, stop=True),
                        r=[("V", part[4]), ("PT", pt)], w=[("ps", bo)])
                    ri = uc[0] % 6
                    uc[0] += 1
                    rc = rcb[64 * (ri // 3):64 * (ri // 3) + 64, ri % 3, 0:W]
                    S.add("dve", lambda e, W=W, nq=nq, bo=bo, rc=rc, db=db: e.tensor_tensor(
                        out=rc.rearrange("p (g q) -> p g q", g=4),
                        in0=ps[bo][db:db + 64, 0:W].rearrange("p (g q) -> p g q", g=4),
                        in1=sinkexp[db:db + 64, :].rearrange("p (g q) -> p g q", g=4)[:, :, 0:nq], op=ALU.add),
                        r=[("ps", bo), ("sinkexp",)], w=[("rcb", ri)])
                    return (u, bo, ri, rc)

                def emit_C1(st):
                    u, bo, ri, rc = st
                    S.add("act", lambda e, rc=rc: e.activation(out=rc, in_=rc, func=AF.Ln),
                          r=[("rcb", ri)], w=[("rcb", ri)])
                    return st

                def emit_C2(st):
                    u, bo, ri, rc = st
                    q0, nq, full, part = u
                    S.add("act", lambda e, rc=rc: e.activation(out=rc, in_=rc, func=AF.Exp, scale=-1.0),
                          r=[("rcb", ri)], w=[("rcb", ri)])
                    for par in range(2):
                        S.add("dve", lambda e, nq=nq, bo=bo, rc=rc, par=par, q0=q0, nb=nb, a0=a0: e.tensor_tensor(
                            out=X4[par * 64:par * 64 + 64, a0:a0 + 2, q0:q0 + nq],
                            in0=ps[bo][nb:nb + 64, par * 2 * nq:(par + 1) * 2 * nq].rearrange("p (o q) -> p o q", o=2),
                            in1=rc[:, par * 2 * nq:(par + 1) * 2 * nq].rearrange("p (o q) -> p o q", o=2),
                            op=ALU.mult),
                            r=[("ps", bo), ("rcb", ri)],
                            w=segkeys("X4", q0, q0 + nq, a0) + segkeys("X4", q0, q0 + nq, a0 + 1))

                nun = len(units)
                stA, stB, stC = {}, {}, {}
                for t in range(nun + 4):
                    if 0 <= t - 4 < nun:
                        emit_C2(stC.pop(t - 4))
                    if 0 <= t - 3 < nun:
                        stC[t - 3] = emit_C1(stB.pop(t - 3))
                    if t < nun:
                        stA[t] = emit_front(units[t], t)
                    if 0 <= t - 1 < nun:
                        stB[t - 1] = emit_B(stA.pop(t - 1))
                while nxtq:
                    nxtq.pop(0)()
                if kvh % 2 == 1:
                    osl = [next_slot(("wo", jl, kvh - 1)), next_slot(("wo", jl, kvh))]
                    nxt = qproj_jobs(kvh + 1, False) if kvh + 1 < 4 else []
                    saved = (bank_map["OW"], bank_map["A4"])
                    if nxt:
                        qdone.add(kvh + 1)
                        bank_map["OW"], bank_map["A4"] = [4, 5, 0, 1], [2, 3]
                    cnt_items = 0
                    for (c0, c1) in BLK_B:
                        N = c1 - c0
                        for m in range(KC):
                            bo = bank("OW")
                            for kk in range(2):
                                for o in range(2):
                                    S.add("pe", lambda e, m=m, o=o, kk=kk, c0=c0, c1=c1, N=N, bo=bo, osl=osl: e.matmul(
                                        ps[bo][:, 0:N], lhsT=ring[:, osl[kk], (m * 2 + o) * 128:(m * 2 + o + 1) * 128],
                                        rhs=X4[:, 2 * kk + o, c0:c1], start=(kk == 0 and o == 0), stop=(kk == 1 and o == 1)),
                                        r=[("ring", osl[kk])] + segkeys("X4", c0, c1, 2 * kk + o), w=[("ps", bo)])
                            x_update(bo, (c0, c1), m, g1)
                            cnt_items += 1
                            if cnt_items % 4 == 0 and nxt:
                                nxt.pop(0)()
                    while nxt:
                        nxt.pop(0)()
                    bank_map["OW"], bank_map["A4"] = saved
            ffn(i, BLK_B)

        import os
        stop = os.environ.get("KSTOP", "all")
        order = ["pro", "a0", "a1", "kv", "b2", "all"]
        lvl = order.index(stop)
        if lvl >= 1:
            layer_a(0)
        if lvl >= 2:
            layer_a(1)
        if lvl >= 3:
            kv_phase()
        if lvl >= 4:
            layer_b(2)
        if lvl >= 5:
            layer_b(3)
        yT_dv = yT_d.rearrange("(c p) t -> p c t", p=128)
        for bi, (c0, c1) in enumerate(BLK_B):
            d0 = 0 if c0 == 0 else 16 + (c0 - C_OWN)
            S.add("sp", lambda e, c0=c0, c1=c1, d0=d0: e.dma_start(
                out=yT_dv[:, :, d0:d0 + (c1 - c0)], in_=xT[:, :, c0:c1]),
                r=[k for kc in range(KC) for k in segkeys("x", c0, c1, kc)], dma="o_y%d" % bi)
        assert lvl < 5 or slot_i[0] == N_SLOTS, (slot_i[0], N_SLOTS)
        print('ops', len(S.ops), 'slots', slot_i[0])

        semkeys = S.analyze()
        sems = {sk: es.enter_context(nc.semaphore("s%d" % n)) for n, sk in enumerate(semkeys)}
        block = es.enter_context(nc.Block())

        def emit_engine(engname):
            def body(e):
                waited = {}
                for k, (eng, fn, r, w, dma) in enumerate(S.ops):
                    if eng != engname:
                        continue
                    for d in S.need_all[k]:
                        sk, val = S.event[d]
                        if waited.get(sk, 0) < val:
                            e.wait_ge(sems[sk], val)
                            waited[sk] = val
                    inst = fn(e)
                    if k in S.event:
                        sk, val = S.event[k]
                        inst.then_inc(sems[sk], 16 if dma is not None else 1)
                if engname == "sp":
                    for name, val in S.final_dma.items():
                        if name.startswith("o_"):
                            e.wait_ge(sems[("dma", name)], val)
            return body

        block.tensor(emit_engine("pe"))
        block.scalar(emit_engine("act"))
        block.vector(emit_engine("dve"))
        block.gpsimd(emit_engine("pool"))
        block.sync(emit_engine("sp"))
    return nc


SLOT_DESCS = []
N_SLOTS = 257


def build_wstream(inp):
    assert len(SLOT_DESCS) == N_SLOTS, len(SLOT_DESCS)
    slots = np.zeros((N_SLOTS, 128, 2048), np.float32)
    w_ada = [np.asarray(inp["w_ada"][i]).reshape(8, 128, 48, 128) for i in range(4)]
    w_ada.append(np.asarray(inp["w_ada_kv"]).reshape(8, 128, 16, 128))
    Win = [np.asarray(inp["w_ffn_in"][i]).reshape(8, 128, 2, NJ, 128) for i in range(4)]
    Wout = [np.asarray(inp["w_ffn_out"][i]).reshape(NJ, 128, 1024) for i in range(4)]
    wkv = np.asarray(inp["w_kv"])
    Wk = wkv[:, :256].reshape(8, 128, 4, 64)
    Wkd = np.concatenate([Wk, Wk], axis=3)
    for k, d in enumerate(SLOT_DESCS):
        kind = d[0]
        if kind == "ada":
            _, li, obp = d
            a = w_ada[li][:, :, 2 * obp:2 * obp + 2, :].transpose(1, 2, 0, 3)
        elif kind == "pool":
            wp = np.asarray(inp["w_pool"][d[1]]).reshape(4, 2, 128, 2, 128)
            a = wp.transpose(2, 0, 3, 1, 4)
        elif kind == "ffn_in":
            _, i, j = d
            a = Win[i][:, :, :, j, :].transpose(1, 2, 0, 3)
        elif kind == "ffn_out":
            _, i, jp = d
            a = Wout[i][jp:jp + 2].transpose(1, 0, 2)
        elif kind == "wk":
            h = d[1]
            a = Wkd[:, :, 2 * h:2 * h + 2].transpose(1, 2, 0, 3)
        elif kind == "wv":
            a = wkv[:, 256:].reshape(8, 128, 256).transpose(1, 0, 2)
        elif kind == "wq":
            _, jl, kvh = d
            Wq = np.asarray(inp["w_q"][jl]).reshape(8, 128, 8, 128)
            a = Wq[:, :, 2 * kvh:2 * kvh + 2, :].transpose(1, 2, 0, 3)
        elif kind == "wo":
            _, jl, kvh = d
            Wo = np.asarray(inp["w_o"][jl]).reshape(8, 128, 8, 128)
            a = Wo[2 * kvh:2 * kvh + 2].transpose(1, 2, 0, 3)
        else:
            raise ValueError(d)
        a = a.reshape(128, -1)
        slots[k, :, :a.shape[1]] = a
    return slots


def fm(v):
    v = np.asarray(v, np.float32)
    lead = v.shape[:-1]
    a = v.reshape(lead + (8, 128))
    a = np.moveaxis(a, -1, 0)
    return a.reshape(128, -1)


_CACHE = {}


def prep_inputs(inp):
    if "nc" not in _CACHE:
        _CACHE["nc"] = build_program()
    ws = build_wstream(inp)
    xp, xs = inp["x_prompt"], inp["x_sample"]
    in_maps = []
    for r in range(NCORES):
        pb, s, sb = r // 4, r % 4, r
        xT = np.zeros((D, T), np.float32)
        xT[:, 0:16] = xs[sb].T
        if s > 0:
            xT[:, 16:176] = xp[pb, s * NOWN - 160:s * NOWN].T
        xT[:, 176:] = xp[pb, s * NOWN:(s + 1) * NOWN].T
        cT = np.stack([fm(inp["c_prompt"][pb]), fm(inp["c_sample"][sb])], axis=2).reshape(128, 16)
        vecs = np.zeros((128, NVEC), np.float32)
        for i in range(4):
            vecs[:, V_BADA + i * 48:V_BADA + (i + 1) * 48] = inp["b_ada"][i].reshape(48, 128).T
            vecs[:, V_GMIX + i * 8:V_GMIX + (i + 1) * 8] = fm(inp["g_mix"][i])
            vecs[:, V_GFFN + i * 8:V_GFFN + (i + 1) * 8] = fm(inp["g_ffn"][i])
        for i in range(2):
            vecs[:, V_PSC + i * 8:V_PSC + (i + 1) * 8] = fm(inp["pool_scale"][i])
            vecs[:, V_GQ + i] = np.tile(inp["g_q"][i], 2)
            vecs[:, V_SINK + i * 16:V_SINK + (i + 1) * 16] = inp["sinks"][i][None, :]
        vecs[:, V_GKV:V_GKV + 8] = fm(inp["g_kv"])
        vecs[:, V_BKV:V_BKV + 16] = inp["b_ada_kv"].reshape(16, 128).T
        vecs[:, V_GK] = np.tile(inp["g_k"], 2)
        vecs[:, V_FLAG] = 1.0 if s > 0 else 0.0
        pos = s * NOWN + np.arange(16)
        for kc in range(8):
            w = 2 ** (kc // 2 + 1)
            vecs[:, V_INVC + kc * 16:V_INVC + (kc + 1) * 16] = (1.0 / np.minimum(w, pos + 1))[None, :]
        vecs[:, V_EPS] = EPS
        spT = np.zeros((128, 2, 8, 16), np.float32)
        sp = inp["state_pool"][:, sb]
        spT[:, :, :, 1:] = sp.reshape(2, 15, 8, 128).transpose(3, 0, 2, 1)
        ck = inp["cache_k"][sb]
        ckT = np.tile(ck.transpose(2, 1, 0), (2, 1, 1)).reshape(128, 512)
        cvr = inp["cache_v"][sb].reshape(128, 256)
        cv = np.ones((128, 384), np.float32)
        cv[:, 0:64] = cvr[:, 0:64]
        cv[:, 128:192] = cvr[:, 64:128]
        cv[:, 192:256] = cvr[:, 128:192]
        cv[:, 320:384] = cvr[:, 192:256]
        in_maps.append({"xT": xT, "cT": np.ascontiguousarray(cT), "vecs": vecs,
                        "spT": spT.reshape(128, 256), "ckT": np.ascontiguousarray(ckT), "cv": cv,
                        "cvraw": np.ascontiguousarray(cvr), "wstream": ws})
    return in_maps


def assemble(R, inp):
    xp = inp["x_prompt"]
    B, L = xp.shape[0], xp.shape[1]
    y_p = np.zeros((B, L, D), np.float32)
    y_s = np.zeros((NCORES, 16, D), np.float32)
    pool_p = np.zeros((2, B, 15, D), np.float32)
    k_p = np.zeros((B, 128, 4, 64), np.float32)
    v_p = np.zeros((B, 128, 4, 64), np.float32)
    pool_s = np.zeros((2, NCORES, 15, D), np.float32)
    k_s = np.zeros((NCORES, 128, 4, 64), np.float32)
    v_s = np.zeros((NCORES, 128, 4, 64), np.float32)
    for r in range(NCORES):
        pb, s, sb = r // 4, r % 4, r
        o = R[r]
        yT = np.asarray(o["yT"])
        y_s[sb] = yT[:, 0:16].T
        y_p[pb, s * NOWN:(s + 1) * NOWN] = yT[:, 16:].T
        pT = np.asarray(o["poolT"])
        for i in range(2):
            pool_s[i, sb] = pT[i][:, 1:16].T
            if s == 3:
                pool_p[i, pb] = pT[i][:, 17:32].T
        k_s[sb] = np.asarray(o["ksT"]).transpose(2, 0, 1)
        v_s[sb] = np.asarray(o["vs"]).reshape(128, 4, 64)
        if s == 3:
            k_p[pb] = np.asarray(o["knT"])[:, :, 16:144].transpose(2, 0, 1)
            v_p[pb] = np.asarray(o["vn"])[16:144].reshape(128, 4, 64)
    return (y_p, y_s, pool_p, k_p, v_p, pool_s, k_s, v_s)


def kernel(**inputs):
    inp = {k: np.asarray(v) for k, v in inputs.items()}
    if "nc" not in _CACHE:
        _CACHE["nc"] = build_program()
    nc = _CACHE["nc"]
    in_maps = prep_inputs(inp)
    res = run_bass_kernel_spmd(nc, in_maps, core_ids=list(range(NCORES)))
    return assemble(res.results, inp)
```

```python
import numpy as np
from contextlib import ExitStack
import concourse.bass as bass
import concourse.mybir as mybir
from concourse.bass_utils import run_bass_kernel_spmd

F32 = mybir.dt.float32
BF16 = mybir.dt.bfloat16
AF = mybir.ActivationFunctionType
ALU = mybir.AluOpType

NCORES = 8
D = 1024
KC = 8
T = 2224
C_OWN = 176
NOWN = 2048
SEGB = [0, 16, 176, 688, 1200, 1712, 2224]
BLK_A = [(0, 176), (176, 688), (688, 1200), (1200, 1712), (1712, 2224)]
BLK_B = [(0, 16), (176, 688), (688, 1200), (1200, 1712), (1712, 2224)]
NJ = 22
GROUPS = [(0, 4), (4, 8), (8, 12), (12, 16), (16, 20), (20, 22)]
NR = 4
PTL = 2256
EPS = 1e-6
ROT = 2000
NV_TILES = 19

V_BADA = 0
V_GMIX = 192
V_GFFN = 224
V_PSC = 256
V_GKV = 272
V_BKV = 280
V_GQ = 296
V_GK = 298
V_SINK = 299
V_FLAG = 331
V_INVC = 332
V_EPS = 460
NVEC = 461


def segkeys(name, c0, c1, *extra):
    ks = []
    for i in range(len(SEGB) - 1):
        if c0 < SEGB[i + 1] and c1 > SEGB[i]:
            ks.append((name,) + tuple(extra) + (i,))
    return ks


def colsegs(c0, c1):
    out = []
    if c0 < 16:
        out.append((c0, min(c1, 16), 1))
    if c1 > 16:
        out.append((max(c0, 16), c1, 0))
    return out


class Sched:
    def __init__(self):
        self.ops = []

    enabled = True

    def add(self, eng, fn, r=(), w=(), dma=None):
        if not self.enabled:
            return -1
        self.ops.append((eng, fn, tuple(r), tuple(w), dma))
        return len(self.ops) - 1

    def analyze(self):
        ops = self.ops
        last_w = {}
        readers = {}
        need_all = []
        signal = [False] * len(ops)
        for k, (eng, fn, r, w, dma) in enumerate(ops):
            raw = set()
            oth = set()
            for key in r:
                if key in last_w:
                    raw.add(last_w[key])
                if key[0] == "ps":
                    for idx in readers.get(key, {}).values():
                        oth.add(idx)
            for key in w:
                if key in last_w:
                    oth.add(last_w[key])
                for idx in readers.get(key, {}).values():
                    oth.add(idx)
            for key in r:
                readers.setdefault(key, {})[eng if dma is None else ("dma", k)] = k
            for key in w:
                last_w[key] = k
                readers[key] = {}
            need_dma = set()
            need_eng = {}
            for d in raw | oth:
                if d == k:
                    continue
                de, ddma = ops[d][0], ops[d][4]
                if ddma is not None:
                    need_dma.add(d)
                elif de != eng or dma is not None:
                    need_eng[de] = max(need_eng.get(de, -1), d)
                elif eng in ("act", "dve", "pool"):
                    need_eng[de] = max(need_eng.get(de, -1), d)
            need = sorted(need_dma | set(need_eng.values()))
            for d in need:
                signal[d] = True
            need_all.append(need)
        cnt = {}
        dcnt = {}
        event = {}
        for k, (eng, fn, r, w, dma) in enumerate(ops):
            if dma is not None:
                dcnt[dma] = dcnt.get(dma, 0) + 16
                event[k] = (("dma", dma), dcnt[dma])
            elif signal[k]:
                c = cnt.get(eng, 0)
                cnt[eng] = c + 1
                event[k] = ((eng, c // ROT), c % ROT + 1)
        self.need_all = need_all
        self.event = event
        self.final_dma = dict(dcnt)
        semkeys = set(sk for sk, _ in event.values())
        return sorted(semkeys, key=str)


def build_program():
    nc = bass.Bass("TRN2", target_bir_lowering=False)
    dt = nc.dram_tensor
    xT_d = dt("xT", [D, T], F32, kind="ExternalInput").ap()
    cT_d = dt("cT", [128, 16], F32, kind="ExternalInput").ap()
    vecs_d = dt("vecs", [128, NVEC], F32, kind="ExternalInput").ap()
    spT_d = dt("spT", [128, 256], F32, kind="ExternalInput").ap()
    ckT_d = dt("ckT", [128, 512], F32, kind="ExternalInput").ap()
    cv_d = dt("cv", [128, 384], F32, kind="ExternalInput").ap()
    cvraw_d = dt("cvraw", [128, 256], F32, kind="ExternalInput").ap()
    ws_d = dt("wstream", [N_SLOTS, 128, 2048], F32, kind="ExternalInput").ap()
    yT_d = dt("yT", [D, 16 + NOWN], F32, kind="ExternalOutput").ap()
    poolT_d = dt("poolT", [2, D, 32], F32, kind="ExternalOutput").ap()
    knT_d = dt("knT", [4, 64, 144], F32, kind="ExternalOutput").ap()
    vn_d = dt("vn", [144, 256], F32, kind="ExternalOutput").ap()
    ksT_d = dt("ksT", [4, 64, 128], F32, kind="ExternalOutput").ap()
    vs_d = dt("vs", [128, 256], F32, kind="ExternalOutput").ap()

    S = Sched()
    es = ExitStack()
    with es:
        off = [0]
        NW = 53200
        arena = es.enter_context(nc.sbuf_tensor("arena", [128, NW], F32))

        def carve(nbytes):
            n = (nbytes + 3) // 4
            a = arena[:, off[0]:off[0] + n]
            off[0] += n
            return a

        xT = carve(KC * T * 4).rearrange("p (c t) -> p c t", c=KC)
        rstd = carve(T * 4)
        ring = carve(NR * 2048 * 2).bitcast(BF16).rearrange("p (s n) -> p s n", s=NR)
        r1raw = carve(KC * T * 2)
        R1 = r1raw.bitcast(BF16).rearrange("p (c t) -> p c t", c=KC)
        PA = r1raw[:, 0:PTL]
        PB = r1raw[:, PTL:2 * PTL]
        PC = r1raw[:, 2 * PTL:3 * PTL]
        X4 = carve(4 * T * 2).bitcast(BF16).rearrange("p (c t) -> p c t", c=4)
        kv_off = off[0]
        kT = carve(4 * T * 2).bitcast(BF16).rearrange("p (c t) -> p c t", c=4)
        Vt = carve(NV_TILES * 384 * 2).bitcast(BF16).rearrange("p (j n) -> p j n", j=NV_TILES)
        assert off[0] - kv_off >= 3 * PTL
        PSETS = [(PA, PB, PC), tuple(arena[:, kv_off + k * PTL:kv_off + (k + 1) * PTL] for k in range(3))]
        tmpf = carve(4 * 512 * 4).rearrange("p (i n) -> p i n", i=4)
        sqb = carve(2 * 512 * 2).bitcast(BF16).rearrange("p (i n) -> p i n", i=2)
        PTb = carve(2 * 512 * 2).bitcast(BF16).rearrange("p (i n) -> p i n", i=2)
        rcb = carve(3 * 256 * 4).rearrange("p (i n) -> p i n", i=3)
        sinkexp = carve(256 * 4)
        vecs = carve(NVEC * 4)
        cTs = carve(16 * 4)
        cact = carve(16 * 2).bitcast(BF16).rearrange("p (c v) -> p c v", v=2)
        ada = carve(5 * 96 * 4).rearrange("p (l o v) -> p l o v", l=5, v=2)
        der = carve(4 * 64 * 4 + 64)
        ones = carve(128 * 2).bitcast(BF16)
        bones = carve(128 * 2).bitcast(BF16)
        zer = carve(64 * 4)
        spT = carve(256 * 4).rearrange("p (i c r) -> p i c r", i=2, c=8)
        poolout = carve(2 * 8 * 32 * 4).rearrange("p (i c r) -> p i c r", i=2, c=8)
        knout = carve(4 * 144 * 4).rearrange("p (k c) -> p k c", k=4)
        vnout = carve(2 * 256 * 4).rearrange("p (i n) -> p i n", i=2)
        ckT = carve(512 * 2).bitcast(BF16).rearrange("p (k n) -> p k n", k=4)
        print('arena words used', off[0], 'of', NW)
        assert off[0] <= NW, off[0]

        ps = [es.enter_context(nc.psum_tensor("ps%d" % i, [128, 512], F32)) for i in range(8)]
        bank_ctr = {"A": 0, "B": 0, "O": 0, "O3": 0, "S": 0, "OW": 0, "A4": 0}
        bank_map = {"A": [0, 1], "B": [2, 3], "O": [4, 5], "O3": [4, 5, 6, 7], "S": [6, 7], "OW": [4, 5, 0, 1, 2, 3], "A4": [0, 1, 2, 3]}

        def bank(role):
            lst = bank_map[role]
            b = lst[bank_ctr[role] % len(lst)]
            bank_ctr[role] += 1
            return b

        def dG1(i):
            return der[:, i * 64:i * 64 + 16].rearrange("p (c v) -> p c v", v=2)

        def dG2(i):
            return der[:, i * 64 + 16:i * 64 + 32].rearrange("p (c v) -> p c v", v=2)

        def dGP(i):
            return der[:, i * 64 + 32:i * 64 + 48].rearrange("p (c v) -> p c v", v=2)

        def dG1h(i):
            return der[:, i * 64 + 48:i * 64 + 56]

        def dS1h(i):
            return der[:, i * 64 + 56:i * 64 + 64]

        dGkv = der[:, 256:272].rearrange("p (c v) -> p c v", v=2)
        eps_col = vecs[:, V_EPS:V_EPS + 1]
        flag_col = vecs[:, V_FLAG:V_FLAG + 1]

        def adav(i, q):
            return ada[:, i, q * 8:(q + 1) * 8, :]

        slot_i = [0]

        SLOT_DESCS.clear()

        def next_slot(desc):
            k = slot_i[0]
            slot_i[0] += 1
            SLOT_DESCS.append(desc)
            s = k % NR
            S.add("pool", lambda e, k=k, s=s: e.dma_start(out=ring[:, s, :], in_=ws_d[k]),
                  w=[("ring", s)], dma="ring%d" % s)
            return s

        xT_dv = xT_d.rearrange("(c p) t -> p c t", p=128)
        for kc in range(KC):
            S.add("sp", lambda e, kc=kc: e.dma_start(out=xT[:, kc, :], in_=xT_dv[:, kc, :]),
                  w=segkeys("x", 0, T, kc), dma="ldx%d" % kc)
        S.add("sp", lambda e: e.dma_start(out=vecs, in_=vecs_d), w=[("vecs",)], dma="ldvec")
        S.add("sp", lambda e: e.dma_start(out=cTs, in_=cT_d), w=[("cTs",)], dma="ldc")
        S.add("sp", lambda e: e.dma_start(out=spT.rearrange("p i c r -> p (i c r)"), in_=spT_d),
              w=[("spT",)], dma="ldsp")
        S.add("pool", lambda e: e.dma_start(out=ckT.rearrange("p k n -> p (k n)"), in_=ckT_d),
              w=[("ckT",)], dma="ldck")
        ckT_dv = ckT_d.rearrange("p (k n) -> p k n", k=4)
        S.add("sp", lambda e: e.dma_start(out=ksT_d.rearrange("k d c -> d k c")[:, :, 0:112],
                                          in_=ckT_dv[0:64, :, 16:128]), dma="o_ks0")
        S.add("sp", lambda e: e.dma_start(out=vs_d[0:112, :], in_=cvraw_d[16:128, :]), dma="o_vs0")

        import os
        kpro = int(os.environ.get("KPRO", "9"))
        S.enabled = kpro >= 1
        S.add("dve", lambda e: e.memset(ones, 1.0), w=[("ones",)])
        S.add("dve", lambda e: e.memset(bones, 0.0), w=[("bones",)])
        S.add("dve", lambda e: e.memset(bones[0:64, 0:64], 1.0), w=[("bones",)])
        S.add("dve", lambda e: e.memset(bones[64:128, 64:128], 1.0), w=[("bones",)])
        S.add("dve", lambda e: e.memset(zer, 0.0), w=[("zer",)])
        S.add("act", lambda e: e.activation(out=cact.rearrange("p c v -> p (c v)"), in_=cTs, func=AF.Silu),
              r=[("cTs",)], w=[("cact",)])

        bz = 7
        DERc = [None]

        def DERL(i):
            return [("der", i), ("ada", i), ("vecs",)]

        def ada_job(li, obp):
            def run():
                s = next_slot(("ada", li, obp))
                for o in range(2):
                    ob = 2 * obp + o
                    colbase = li * 96 + ob * 2
                    for kc in range(KC):
                        S.add("pe", lambda e, s=s, o=o, kc=kc, cb=colbase: e.matmul(
                            ps[bz][:, cb:cb + 2], lhsT=ring[:, s, (o * 8 + kc) * 128:(o * 8 + kc + 1) * 128],
                            rhs=cact[:, kc, :], start=(kc == 0), stop=(kc == 7)),
                            r=[("ring", s), ("cact",)], w=[("ps", bz)])
            return run

        def ada_finish(li, part=None):
            def run():
                nob = 48 if li < 4 else 16
                boff = V_BADA + li * 48 if li < 4 else V_BKV
                o0, o1 = (0, nob) if part is None else ((0, 24) if part == 0 else (24, 48))
                for v in range(2):
                    S.add("dve", lambda e, v=v: e.tensor_tensor(
                        out=ada[:, li, o0:o1, v],
                        in0=ps[bz][:, li * 96 + 2 * o0:li * 96 + 2 * o1].rearrange("p (o v) -> p o v", v=2)[:, :, v],
                        in1=vecs[:, boff + o0:boff + o1], op=ALU.add),
                        r=[("ps", bz), ("vecs",)], w=[("ada", li)])
                if li == 4:
                    gkv = vecs[:, V_GKV:V_GKV + 8]
                    for v in range(2):
                        S.add("dve", lambda e, v=v: e.scalar_tensor_tensor(
                            out=dGkv[:, :, v], in0=ada[:, 4, 8:16, v], scalar=1.0, in1=gkv, op0=ALU.add, op1=ALU.mult),
                            r=[("ada", 4), ("vecs",)], w=[("der", 4)])
                    return
                i = li
                gm = vecs[:, V_GMIX + i * 8:V_GMIX + i * 8 + 8]
                gf = vecs[:, V_GFFN + i * 8:V_GFFN + i * 8 + 8]
                for v in range(2):
                    if part in (None, 0):
                        S.add("dve", lambda e, v=v: e.scalar_tensor_tensor(
                            out=dG1(i)[:, :, v], in0=adav(i, 1)[:, :, v], scalar=1.0, in1=gm, op0=ALU.add, op1=ALU.mult),
                            r=[("ada", i), ("vecs",)], w=[("der", i)])
                    if part in (None, 1):
                        S.add("dve", lambda e, v=v: e.scalar_tensor_tensor(
                            out=dG2(i)[:, :, v], in0=adav(i, 4)[:, :, v], scalar=1.0, in1=gf, op0=ALU.add, op1=ALU.mult),
                            r=[("ada", i), ("vecs",)], w=[("der", i)])
                    if i < 2 and part in (None, 0):
                        psc = vecs[:, V_PSC + i * 8:V_PSC + i * 8 + 8]
                        S.add("dve", lambda e, v=v, psc=psc: e.tensor_tensor(
                            out=dGP(i)[:, :, v], in0=adav(i, 2)[:, :, v], in1=psc, op=ALU.mult),
                            r=[("ada", i), ("vecs",)], w=[("der", i)])
                if i < 2 and part in (None, 0):
                    S.add("dve", lambda e: e.tensor_scalar(
                        out=dG1h(i), in0=dG1(i)[:, :, 0], scalar1=flag_col, scalar2=None, op0=ALU.mult),
                        r=[("der", i), ("vecs",)], w=[("der", i)])
                    S.add("dve", lambda e: e.tensor_scalar(
                        out=dS1h(i), in0=adav(i, 0)[:, :, 0], scalar1=flag_col, scalar2=None, op0=ALU.mult),
                        r=[("ada", i), ("vecs",)], w=[("der", i)])
            return run

        pending = []

        def queue_ada(li):
            nob = 48 if li < 4 else 16
            for obp in range(nob // 2):
                pending.append(ada_job(li, obp))
            pending.append(ada_finish(li))

        def drain(n):
            for _ in range(min(n, len(pending))):
                pending.pop(0)()


        sq_ctr = [0]

        def norm_stats(blocks, alt=False):
            for (c0, c1) in blocks:
                N = c1 - c0
                b = bank("S")
                tb = 2 + (sq_ctr[0] % 2)
                sq_ctr[0] += 1
                for kc in range(KC):
                    sb = kc % 4
                    sbuf = sqb[:, sb, 0:N] if sb < 2 else PTb[:, sb - 2, 0:N]
                    skey = ("sqb", sb) if sb < 2 else ("PT", sb - 2)
                    if alt and kc % 2 == 1:
                        S.add("dve", lambda e, kc=kc, sbuf=sbuf, c0=c0, c1=c1: e.tensor_tensor(
                            out=sbuf, in0=xT[:, kc, c0:c1], in1=xT[:, kc, c0:c1], op=ALU.mult),
                            r=segkeys("x", c0, c1, kc), w=[skey])
                    else:
                        S.add("act", lambda e, kc=kc, sbuf=sbuf, c0=c0, c1=c1: e.activation(
                            out=sbuf, in_=xT[:, kc, c0:c1], func=AF.Square),
                            r=segkeys("x", c0, c1, kc), w=[skey])
                    S.add("pe", lambda e, kc=kc, sbuf=sbuf, N=N, b=b: e.matmul(
                        ps[b][:, 0:N], lhsT=ones, rhs=sbuf, start=(kc == 0), stop=(kc == 7)),
                        r=[skey, ("ones",)], w=[("ps", b)])
                S.add("act", lambda e, N=N, b=b, tb=tb: e.activation(
                    out=tmpf[:, tb, 0:N], in_=ps[b][:, 0:N], func=AF.Ln, bias=eps_col, scale=1.0 / D),
                    r=[("ps", b), ("vecs",)], w=[("tmp", tb)])
                S.add("act", lambda e, N=N, c0=c0, c1=c1, tb=tb: e.activation(
                    out=rstd[:, c0:c1], in_=tmpf[:, tb, 0:N], func=AF.Exp, scale=-0.5),
                    r=[("tmp", tb)], w=segkeys("rstd", c0, c1))

        ucnt = [0]

        def norm_apply(blocks, Gv, Sv, out):
            for (c0, c1) in blocks:
                for kc in range(KC):
                    for (a, b_, v) in colsegs(c0, c1):
                        n = b_ - a
                        ti = ucnt[0] % 2
                        ucnt[0] += 1
                        S.add("dve", lambda e, kc=kc, a=a, b_=b_, n=n, ti=ti: e.tensor_tensor(
                            out=tmpf[:, ti, 0:n], in0=xT[:, kc, a:b_], in1=rstd[:, a:b_], op=ALU.mult),
                            r=segkeys("x", a, b_, kc) + segkeys("rstd", a, b_), w=[("tmp", ti)])
                        S.add("act", lambda e, kc=kc, a=a, b_=b_, n=n, ti=ti, v=v: e.activation(
                            out=out[:, kc, a:b_], in_=tmpf[:, ti, 0:n], func=AF.Identity,
                            bias=Sv[:, kc, v:v + 1], scale=Gv[:, kc, v:v + 1]),
                            r=[("tmp", ti)] + DERc[0], w=segkeys("R1", a, b_, kc))

        def norm_full(blocks, Gv, Sv, out):
            for bi, blk in enumerate(blocks):
                norm_stats([blk], alt=True)
                if bi >= 1:
                    norm_apply([blocks[bi - 1]], Gv, Sv, out)
            norm_apply([blocks[-1]], Gv, Sv, out)

        hn_ctr = [0]

        def head_norm(b, N, gcol, dest, dest_keys, caps=(), sbank=None):
            k = hn_ctr[0] % 2
            hn_ctr[0] += 1
            sb, tb, bs = k, 2 + k, (bank("S") if sbank is None else sbank)
            S.add("act", lambda e: e.activation(out=sqb[:, sb, 0:N], in_=ps[b][:, 0:N], func=AF.Square),
                  r=[("ps", b)], w=[("sqb", sb)])
            S.add("pe", lambda e: e.matmul(ps[bs][:, 0:N], lhsT=bones, rhs=sqb[:, sb, 0:N], start=True, stop=True),
                  r=[("sqb", sb), ("bones",)], w=[("ps", bs)])
            S.add("act", lambda e: e.activation(
                out=tmpf[:, tb, 0:N], in_=ps[bs][:, 0:N], func=AF.Ln, bias=eps_col, scale=1.0 / 64),
                r=[("ps", bs), ("vecs",)], w=[("tmp", tb)])
            S.add("act", lambda e: e.activation(
                out=tmpf[:, tb, 0:N], in_=tmpf[:, tb, 0:N], func=AF.Exp, scale=-0.5),
                r=[("tmp", tb)], w=[("tmp", tb)])
            S.add("dve", lambda e: e.scalar_tensor_tensor(
                out=dest, in0=ps[b][:, 0:N], scalar=gcol, in1=tmpf[:, tb, 0:N], op0=ALU.mult, op1=ALU.mult),
                r=[("ps", b), ("tmp", tb), ("vecs",)], w=dest_keys)
            for (pa, pb_, cdst, ckeys) in caps:
                S.add("dve", lambda e, pa=pa, pb_=pb_, cdst=cdst: e.scalar_tensor_tensor(
                    out=cdst, in0=ps[b][0:64, pa:pb_], scalar=gcol[0:64, :], in1=tmpf[0:64, tb, pa:pb_],
                    op0=ALU.mult, op1=ALU.mult),
                    r=[("ps", b), ("tmp", tb), ("vecs",)], w=ckeys)

        def x_update(b, blk, m, gvec):
            c0, c1 = blk
            for (a, b_, v) in colsegs(c0, c1):
                S.add("dve", lambda e, b=b, m=m, a=a, b_=b_, v=v, c0=c0: e.scalar_tensor_tensor(
                    out=xT[:, m, a:b_], in0=ps[b][:, a - c0:b_ - c0], scalar=gvec[:, m, v:v + 1],
                    in1=xT[:, m, a:b_], op0=ALU.mult, op1=ALU.add),
                    r=[("ps", b)] + DERc[0] + segkeys("x", a, b_, m), w=segkeys("x", a, b_, m))

        def ffn(i, blocks):
            norm_full(blocks, dG2(i), adav(i, 3), R1)
            g2 = adav(i, 5)
            sgc = [0]
            for (j0, j1) in GROUPS:
                for jj, j in enumerate(range(j0, j1)):
                    s = next_slot(("ffn_in", i, j))
                    for (c0, c1) in blocks:
                        N = c1 - c0
                        bg = bank("A")
                        bu = bank("B")
                        for kc in range(KC):
                            S.add("pe", lambda e, s=s, kc=kc, c0=c0, c1=c1, N=N, bg=bg: e.matmul(
                                ps[bg][:, 0:N], lhsT=ring[:, s, kc * 128:(kc + 1) * 128], rhs=R1[:, kc, c0:c1],
                                start=(kc == 0), stop=(kc == 7)),
                                r=[("ring", s)] + segkeys("R1", c0, c1, kc), w=[("ps", bg)])
                        for kc in range(KC):
                            S.add("pe", lambda e, s=s, kc=kc, c0=c0, c1=c1, N=N, bu=bu: e.matmul(
                                ps[bu][:, 0:N], lhsT=ring[:, s, (8 + kc) * 128:(9 + kc) * 128], rhs=R1[:, kc, c0:c1],
                                start=(kc == 0), stop=(kc == 7)),
                                r=[("ring", s)] + segkeys("R1", c0, c1, kc), w=[("ps", bu)])
                        ti = sgc[0] % 2
                        sgc[0] += 1
                        S.add("act", lambda e, bg=bg, N=N, ti=ti: e.activation(
                            out=tmpf[:, ti, 0:N], in_=ps[bg][:, 0:N], func=AF.Silu),
                            r=[("ps", bg)], w=[("tmp", ti)])
                        S.add("dve", lambda e, bu=bu, N=N, ti=ti, jj=jj, c0=c0, c1=c1: e.tensor_tensor(
                            out=X4[:, jj, c0:c1], in0=tmpf[:, ti, 0:N], in1=ps[bu][:, 0:N], op=ALU.mult),
                            r=[("ps", bu), ("tmp", ti)], w=segkeys("X4", c0, c1, jj))
                    drain(2)
                nj = j1 - j0
                oslots = [next_slot(("ffn_out", i, j0 + 2 * q_)) for q_ in range(nj // 2)]
                for (c0, c1) in blocks:
                    N = c1 - c0
                    for m in range(KC):
                        bo = bank("OW")
                        for jj in range(nj):
                            s = oslots[jj // 2]
                            o = jj % 2
                            S.add("pe", lambda e, s=s, o=o, m=m, jj=jj, c0=c0, c1=c1, N=N, bo=bo, nj=nj: e.matmul(
                                ps[bo][:, 0:N], lhsT=ring[:, s, o * 1024 + m * 128:o * 1024 + (m + 1) * 128],
                                rhs=X4[:, jj, c0:c1], start=(jj == 0), stop=(jj == nj - 1)),
                                r=[("ring", s)] + segkeys("X4", c0, c1, jj), w=[("ps", bo)])
                        x_update(bo, (c0, c1), m, g2)
            drain(len(pending))

        def ptc(t):
            return 16 + t if t < 16 else 32 + t

        def layer_a(i):
            DERc[0] = DERL(i)
            if i == 0:
                for obp in range(12):
                    pending.append(ada_job(0, obp))
                pending.append(ada_finish(0, part=0))
                drain(len(pending))
                norm_stats(BLK_A)
                for obp in range(12, 24):
                    pending.append(ada_job(0, obp))
                drain(len(pending))
                pending.append(ada_finish(0, part=1))
            else:
                norm_stats(BLK_A)
            wps = next_slot(("pool", i))
            G1, S1, G1h, S1h, GP = dG1(i), adav(i, 0), dG1h(i), dS1h(i), dGP(i)

            def stage1(kc):
                k = kc % 2
                PA_, PB_, PC_ = PSETS[k]
                ka, kc_ = ("pa", k), ("pc", k)
                S.add("dve", lambda e: e.tensor_copy(out=PA_[:, 0:16], in_=spT[:, i, kc, :]), r=[("spT",)], w=[ka])
                S.add("dve", lambda e: e.memset(PA_[:, 32:48], 0.0), w=[ka])
                S.add("dve", lambda e: e.tensor_tensor(
                    out=PC_[:, 16:32], in0=xT[:, kc, 0:16], in1=rstd[:, 0:16], op=ALU.mult),
                    r=segkeys("x", 0, 16, kc) + segkeys("rstd", 0, 16), w=[kc_])
                S.add("dve", lambda e: e.tensor_tensor(
                    out=PC_[:, 48:PTL], in0=xT[:, kc, 16:T], in1=rstd[:, 16:T], op=ALU.mult),
                    r=segkeys("x", 16, T, kc) + segkeys("rstd", 16, T), w=[kc_])
                S.add("act", lambda e: e.activation(
                    out=PA_[:, 16:32], in_=PC_[:, 16:32], func=AF.Identity,
                    bias=S1[:, kc, 1:2], scale=G1[:, kc, 1:2]), r=[kc_] + DERc[0], w=[ka])
                S.add("act", lambda e: e.activation(
                    out=PA_[:, 48:208], in_=PC_[:, 48:208], func=AF.Identity,
                    bias=S1h[:, kc:kc + 1], scale=G1h[:, kc:kc + 1]), r=[kc_] + DERc[0], w=[ka])
                S.add("act", lambda e: e.activation(
                    out=PA_[:, 208:PTL], in_=PC_[:, 208:PTL], func=AF.Identity,
                    bias=S1[:, kc, 0:1], scale=G1[:, kc, 0:1]), r=[kc_] + DERc[0], w=[ka])
                S.add("act", lambda e: e.activation(
                    out=poolout[:, i, kc, 0:16], in_=PA_[:, 16:32], func=AF.Identity), r=[ka], w=[("poolout",)])
                S.add("act", lambda e: e.activation(
                    out=poolout[:, i, kc, 16:32], in_=PA_[:, PTL - 16:PTL], func=AF.Identity), r=[ka], w=[("poolout",)])

            def stage2(kc):
                k = kc % 2
                g = kc // 2
                PA_, PB_, PC_ = PSETS[k]
                ka = ("pa", k)
                dslot = 2 * (g % 2) + (kc % 2)
                seq = [(PA_, ("pa", k)), (PB_, ("pb", k)), (PC_, ("pc", k)), (PB_, ("pb", k)), (PC_, ("pc", k))]
                cur, curk = seq[0]
                for L in range(1, g + 2):
                    st = 2 ** (L - 1)
                    lo = 2 ** L - 1
                    dst, dstk = seq[L]
                    S.add("dve", lambda e, cur=cur, dst=dst, st=st, lo=lo: e.tensor_tensor(
                        out=dst[:, lo:PTL], in0=cur[:, lo:PTL], in1=cur[:, lo - st:PTL - st], op=ALU.add),
                        r=[curk], w=[dstk])
                    cur, curk = dst, dstk
                inv = 1.0 / (2 ** (g + 1))
                S.add("dve", lambda e: e.scalar_tensor_tensor(
                    out=X4[:, dslot, 0:16], in0=cur[:, 16:32], scalar=inv, in1=PA_[:, 16:32],
                    op0=ALU.mult, op1=ALU.subtract), r=[curk, ka], w=segkeys("X4", 0, 16, dslot))
                S.add("dve", lambda e: e.scalar_tensor_tensor(
                    out=X4[:, dslot, 16:T], in0=cur[:, 48:PTL], scalar=inv, in1=PA_[:, 48:PTL],
                    op0=ALU.mult, op1=ALU.subtract), r=[curk, ka], w=segkeys("X4", 16, T, dslot))
                S.add("dve", lambda e: e.tensor_tensor(
                    out=tmpf[:, 3, 0:16], in0=cur[:, 208:224], in1=vecs[:, V_INVC + kc * 16:V_INVC + kc * 16 + 16],
                    op=ALU.mult), r=[curk, ("vecs",)], w=[("tmp", 3)])
                S.add("dve", lambda e: e.tensor_tensor(
                    out=X4[:, dslot, 176:192], in0=tmpf[:, 3, 0:16], in1=PA_[:, 208:224], op=ALU.subtract),
                    r=[("tmp", 3), ka], w=segkeys("X4", 176, 192, dslot))

            def group_out(g):
                for ec in range(2):
                    m = 2 * g + ec
                    for (c0, c1) in BLK_A:
                        N = c1 - c0
                        bo = bank("OW")
                        for kk in range(2):
                            dslot = 2 * (g % 2) + kk
                            wo = ((g * 2 + ec) * 2 + kk) * 128
                            S.add("pe", lambda e, wo=wo, dslot=dslot, kk=kk, c0=c0, c1=c1, N=N, bo=bo: e.matmul(
                                ps[bo][:, 0:N], lhsT=ring[:, wps, wo:wo + 128], rhs=X4[:, dslot, c0:c1],
                                start=(kk == 0), stop=(kk == 1)),
                                r=[("ring", wps)] + segkeys("X4", c0, c1, dslot), w=[("ps", bo)])
                        x_update(bo, (c0, c1), m, GP)

            stage1(0)
            for kc in range(KC):
                if kc + 1 < KC:
                    stage1(kc + 1)
                stage2(kc)
                if kc % 2 == 1:
                    if i == 0 and kc == 3:
                        drain(1)
                    group_out(kc // 2)
            drain(len(pending))
            if i == 0:
                queue_ada(1)
                queue_ada(4)
            else:
                queue_ada(2)
            ffn(i, BLK_A)

        def kv_phase():
            DERc[0] = DERL(4)
            alias = [("pa", 1), ("pb", 1), ("pc", 1)]
            S.add("dve", lambda e: e.memset(Vt[:, 0:17, 64:128], 1.0), w=[("V", j) for j in range(17)] + alias)
            S.add("dve", lambda e: e.memset(Vt[:, 0:17, 256:320], 1.0), w=[("V", j) for j in range(17)] + alias)
            S.add("dve", lambda e: e.memset(Vt[0:32, 17, :], 0.0), w=[("V", 17)] + alias)
            S.add("dve", lambda e: e.memset(Vt[0:16, 17, 64:128], 1.0), w=[("V", 17)] + alias)
            S.add("dve", lambda e: e.memset(Vt[0:16, 17, 256:320], 1.0), w=[("V", 17)] + alias)
            S.add("pool", lambda e: e.dma_start(out=Vt[:, 18, :], in_=cv_d), w=[("V", 18)] + alias, dma="ldcv")
            norm_full(BLK_A, dGkv, ada[:, 4, 0:8, :], R1)
            gk = vecs[:, V_GK:V_GK + 1]
            ks = [next_slot(("wk", 0)), next_slot(("wk", 1))]
            vsl = next_slot(("wv",))
            def v_job(j, t0, nt, idx):
                bz_ = 4 + (idx % 2)
                mt = max(nt, 32)
                for kc in range(KC):
                    S.add("pe", lambda e, kc=kc: e.matmul(
                        ps[bz_][0:mt, 0:256], lhsT=R1[:, kc, t0:t0 + mt], rhs=ring[:, vsl, kc * 256:(kc + 1) * 256],
                        start=(kc == 0), stop=(kc == 7)),
                        r=[("ring", vsl)] + segkeys("R1", t0, t0 + mt, kc), w=[("ps", bz_)])
                pv4 = ps[bz_][0:nt, 0:256].rearrange("p (k d) -> p k d", k=4)
                S.add("dve", lambda e: e.tensor_copy(
                    out=Vt[0:nt, j, 0:192].rearrange("p (k d) -> p k d", d=64)[:, 0:3:2, :], in_=pv4[:, 0:2, :]),
                    r=[("ps", bz_)], w=[("V", j)])
                S.add("dve", lambda e: e.tensor_copy(
                    out=Vt[0:nt, j, 192:384].rearrange("p (k d) -> p k d", d=64)[:, 0:3:2, :], in_=pv4[:, 2:4, :]),
                    r=[("ps", bz_)], w=[("V", j)])
                if j == 16:
                    S.add("dve", lambda e: e.tensor_copy(out=vnout[:, 0, :], in_=ps[bz_][:, 0:256]),
                          r=[("ps", bz_)], w=[("vnout",)])
                if j == 17:
                    S.add("dve", lambda e: e.tensor_copy(out=vnout[0:16, 1, :], in_=ps[bz_][0:16, 0:256]),
                          r=[("ps", bz_)], w=[("vnout",)])
                if j == 0:
                    S.add("dve", lambda e: e.tensor_scalar(
                        out=Vt[:, 0, :], in0=Vt[:, 0, :], scalar1=flag_col, scalar2=None, op0=ALU.mult),
                        r=[("V", 0), ("vecs",)], w=[("V", 0)])

            vjobs = [(17, 0, 16, 0)] + [(j, 48 + 128 * j, 128, j + 1) for j in range(17)]
            khn = []
            for kvh in range(4):
                s = ks[kvh // 2]
                base = (kvh % 2) * 1024
                for (c0, c1) in BLK_A:
                    N = c1 - c0
                    b = bank("A4")
                    for kc in range(KC):
                        S.add("pe", lambda e, s=s, base=base, kc=kc, c0=c0, c1=c1, N=N, b=b: e.matmul(
                            ps[b][:, 0:N], lhsT=ring[:, s, base + kc * 128:base + (kc + 1) * 128],
                            rhs=R1[:, kc, c0:c1], start=(kc == 0), stop=(kc == 7)),
                            r=[("ring", s)] + segkeys("R1", c0, c1, kc), w=[("ps", b)])
                    caps = []
                    if c0 == 0:
                        caps.append((0, 16, knout[0:64, kvh, 0:16], [("knout",)]))
                    if c1 == T:
                        caps.append((N - 128, N, knout[0:64, kvh, 16:144], [("knout",)]))
                    if khn:
                        khn.pop(0)()
                    khn.append(lambda b=b, N=N, kvh=kvh, c0=c0, c1=c1, caps=caps: head_norm(
                        b, N, gk, kT[:, kvh, c0:c1], segkeys("kT", c0, c1, kvh), caps))
                    if vjobs:
                        v_job(*vjobs.pop(0))
            while khn:
                khn.pop(0)()
            while vjobs:
                v_job(*vjobs.pop(0))
            S.add("sp", lambda e: e.dma_start(out=knT_d.rearrange("k d c -> d k c"), in_=knout[0:64, :, :]),
                  r=[("knout",)], dma="o_kn")
            S.add("sp", lambda e: e.dma_start(out=ksT_d.rearrange("k d c -> d k c")[:, :, 112:128],
                                              in_=knout[0:64, :, 0:16]), r=[("knout",)], dma="o_ks1")
            S.add("sp", lambda e: e.dma_start(out=vn_d[16:144, :], in_=vnout[:, 0, :]), r=[("vnout",)], dma="o_vn0")
            S.add("sp", lambda e: e.dma_start(out=vn_d[0:16, :], in_=vnout[0:16, 1, :]), r=[("vnout",)], dma="o_vn1")
            S.add("sp", lambda e: e.dma_start(out=vs_d[112:128, :], in_=vnout[0:16, 1, :]), r=[("vnout",)], dma="o_vs1")
            S.add("sp", lambda e: e.dma_start(out=poolT_d.rearrange("i (c p) r -> p i c r", p=128), in_=poolout),
                  r=[("poolout",)], dma="o_pool")

        VC0 = [0, 64, 192, 256]

        def layer_b(i):
            jl = i - 2
            DERc[0] = DERL(i)
            if i == 2:
                queue_ada(3)
            if i == 2:
                norm_apply(BLK_B, dG1(i), adav(i, 0), R1)
            else:
                norm_full(BLK_B, dG1(i), adav(i, 0), R1)
            g1 = adav(i, 2)
            gq = vecs[:, V_GQ + jl:V_GQ + jl + 1]
            uc = [0]
            rq = rstd.bitcast(BF16).rearrange("p (c t) -> p c t", c=2)
            Qb = [X4, rq]

            def qkeys(qi, c0, c1, o):
                return segkeys("X4", c0, c1, o) if qi == 0 else segkeys("rstd", c0, c1)

            def qproj_jobs(kvh, inline):
                state = {}
                jobs = []

                def mk(o, c0, c1):
                    def run():
                        if "qs" not in state:
                            state["qs"] = next_slot(("wq", jl, kvh))
                        qs = state["qs"]
                        N = c1 - c0
                        b = 7 if inline else bank("A4")
                        for kc in range(KC):
                            S.add("pe", lambda e, kc=kc: e.matmul(
                                ps[b][:, 0:N], lhsT=ring[:, qs, (o * 8 + kc) * 128:(o * 8 + kc + 1) * 128],
                                rhs=R1[:, kc, c0:c1], start=(kc == 0), stop=(kc == 7)),
                                r=[("ring", qs)] + segkeys("R1", c0, c1, kc), w=[("ps", b)])
                        prev = state.pop("hn", None)
                        if prev is not None:
                            prev()
                        state["hn"] = lambda: head_norm(b, N, gq, rq[:, o, c0:c1], qkeys(1, c0, c1, o),
                                                        sbank=(6 if inline else None))
                    return run

                def flush():
                    prev = state.pop("hn", None)
                    if prev is not None:
                        prev()
                for o in range(2):
                    for (c0, c1) in BLK_B:
                        jobs.append(mk(o, c0, c1))
                jobs.append(flush)
                return jobs

            qdone = set()
            for kvh in range(4):
                if kvh not in qdone:
                    for jb in qproj_jobs(kvh, False):
                        jb()
                for blk in range(4):
                    par, o = blk // 2, blk % 2
                    h = kvh * 4 + 2 * o + par
                    S.add("act", lambda e, blk=blk, h=h: e.activation(
                        out=sinkexp[:, blk * 64:(blk + 1) * 64], in_=zer, func=AF.Exp,
                        bias=vecs[:, V_SINK + jl * 16 + h:V_SINK + jl * 16 + h + 1], scale=1.0),
                        r=[("zer",), ("vecs",)], w=[("sinkexp",)])
                nxtq = []
                Qc = rq
                qi = 1
                a0 = 2 * (kvh % 2)
                nb = 0 if kvh % 2 == 0 else 64
                db = 64 - nb
                vc = VC0[kvh]
                units = []
                units.append((0, 16,
                              (lambda hb, kvh=kvh: ckT[hb:hb + 64, kvh, 0:128], 18, [("ckT",)]),
                              (lambda hb, kvh=kvh: kT[hb:hb + 64, kvh, 0:32], 32, 0, 32, 17, segkeys("kT", 0, 32, kvh))))
                for c in range(32):
                    if c % 2 == 0:
                        jf, jh, hh = c // 2, c // 2 + 1, 0
                    else:
                        jf, jh, hh = (c + 1) // 2, (c - 1) // 2, 1
                    cf, ch = 48 + 128 * jf, 48 + 128 * jh
                    units.append((C_OWN + 64 * c, 64,
                                  (lambda hb, cf=cf, kvh=kvh: kT[hb:hb + 64, kvh, cf:cf + 128], jf,
                                   segkeys("kT", cf, cf + 128, kvh)),
                                  (lambda hb, ch=ch, kvh=kvh: kT[hb:hb + 64, kvh, ch:ch + 128], 128, 64 * hh, 64 * hh + 64,
                                   jh, segkeys("kT", ch, ch + 128, kvh))))

                def emit_front(u, ui):
                    q0, nq, full, part = u
                    bpar = [bank("A"), bank("B")]
                    pt = ui % 2
                    for fh in range(2):
                        src = full if fh == 0 else part
                        M = 128 if fh == 0 else part[1]
                        keys = full[2] if fh == 0 else part[5]
                        for o in range(2):
                            for par in range(2):
                                hb = par * 64
                                bb = bpar[par]
                                cbase = fh * 256 + o * nq
                                S.add("pe", lambda e, o=o, hb=hb, q0=q0, nq=nq, bb=bb, src=src, M=M, cbase=cbase, Qc=Qc: e.matmul(
                                    ps[bb][0:M, cbase:cbase + nq], lhsT=src[0](hb), rhs=Qc[hb:hb + 64, o, q0:q0 + nq],
                                    start=True, stop=True),
                                    r=keys + qkeys(qi, q0, q0 + nq, o), w=[("ps", bb)])
                    for par in range(2):
                        bb = bpar[par]
                        S.add("act", lambda e, nq=nq, bb=bb, pt=pt, par=par: e.activation(
                            out=PTb[:, pt, par * 256:(par + 1) * 256].rearrange("p (f x) -> p f x", f=2)[:, :, 0:2 * nq],
                            in_=ps[bb][:, 0:512].rearrange("p (f x) -> p f x", f=2)[:, :, 0:2 * nq],
                            func=AF.Exp, scale=0.125),
                            r=[("ps", bb)], w=[("PT", pt)])
                    return (u, pt)

                def emit_B(st):
                    u, pt = st
                    q0, nq, full, part = u
                    W = 4 * nq
                    bo = bank("O3")
                    r0, r1_ = part[2], part[3]
                    S.add("pe", lambda e, W=W, nq=nq, bo=bo, pt=pt, full=full, vc=vc: e.matmul(
                        ps[bo][:, 0:W].rearrange("p (a x) -> p a x", a=2), lhsT=Vt[:, full[1], vc:vc + 128],
                        rhs=PTb[:, pt, :].rearrange("p (a f x) -> p a f x", a=2, f=2)[:, :, 0, 0:2 * nq],
                        start=True, stop=False),
                        r=[("V", full[1]), ("PT", pt)], w=[("ps", bo)])
                    S.add("pe", lambda e, W=W, nq=nq, bo=bo, pt=pt, part=part, r0=r0, r1_=r1_, vc=vc: e.matmul(
                        ps[bo][:, 0:W].rearrange("p (a x) -> p a x", a=2), lhsT=Vt[r0:r1_, part[4], vc:vc + 128],
                        rhs=PTb[r0:r1_, pt, :].rearrange("p (a f x) -> p a f x", a=2, f=2)[:, :, 1, 0:2 * nq],
                        start=False, stop=True),
                        r=[("V", part[4]), ("PT", pt)], w=[("ps", bo)])
                    ri = uc[0] % 6
                    uc[0] += 1
                    rc = rcb[64 * (ri // 3):64 * (ri // 3) + 64, ri % 3, 0:W]
                    S.add("dve", lambda e, W=W, nq=nq, bo=bo, rc=rc, db=db: e.tensor_tensor(
                        out=rc.rearrange("p (g q) -> p g q", g=4),
                        in0=ps[bo][db:db + 64, 0:W].rearrange("p (g q) -> p g q", g=4),
                        in1=sinkexp[db:db + 64, :].rearrange("p (g q) -> p g q", g=4)[:, :, 0:nq], op=ALU.add),
                        r=[("ps", bo), ("sinkexp",)], w=[("rcb", ri)])
                    return (u, bo, ri, rc)

                def emit_C1(st):
                    u, bo, ri, rc = st
                    S.add("act", lambda e, rc=rc: e.activation(out=rc, in_=rc, func=AF.Ln),
                          r=[("rcb", ri)], w=[("rcb", ri)])
                    return st

                def emit_C2(st):
                    u, bo, ri, rc = st
                    q0, nq, full, part = u
                    S.add("act", lambda e, rc=rc: e.activation(out=rc, in_=rc, func=AF.Exp, scale=-1.0),
                          r=[("rcb", ri)], w=[("rcb", ri)])
                    for par in range(2):
                        S.add("dve", lambda e, nq=nq, bo=bo, rc=rc, par=par, q0=q0, nb=nb, a0=a0: e.tensor_tensor(
                            out=X4[par * 64:par * 64 + 64, a0:a0 + 2, q0:q0 + nq],
                            in0=ps[bo][nb:nb + 64, par * 2 * nq:(par + 1) * 2 * nq].rearrange("p (o q) -> p o q", o=2),
                            in1=rc[:, par * 2 * nq:(par + 1) * 2 * nq].rearrange("p (o q) -> p o q", o=2),
                            op=ALU.mult),
                            r=[("ps", bo), ("rcb", ri)],
                            w=segkeys("X4", q0, q0 + nq, a0) + segkeys("X4", q0, q0 + nq, a0 + 1))

                nun = len(units)
                stA, stB, stC = {}, {}, {}
                for t in range(nun + 4):
                    if 0 <= t - 4 < nun:
                        emit_C2(stC.pop(t - 4))
                    if 0 <= t - 3 < nun:
                        stC[t - 3] = emit_C1(stB.pop(t - 3))
                    if t < nun:
                        stA[t] = emit_front(units[t], t)
                    if 0 <= t - 1 < nun:
                        stB[t - 1] = emit_B(stA.pop(t - 1))
                while nxtq:
                    nxtq.pop(0)()
                if kvh % 2 == 1:
                    osl = [next_slot(("wo", jl, kvh - 1)), next_slot(("wo", jl, kvh))]
                    nxt = qproj_jobs(kvh + 1, False) if kvh + 1 < 4 else []
                    saved = (bank_map["OW"], bank_map["A4"])
                    if nxt:
                        qdone.add(kvh + 1)
                        bank_map["OW"], bank_map["A4"] = [4, 5, 0, 1], [2, 3]
                    cnt_items = 0
                    for (c0, c1) in BLK_B:
                        N = c1 - c0
                        for m in range(KC):
                            bo = bank("OW")
                            for kk in range(2):
                                for o in range(2):
                                    S.add("pe", lambda e, m=m, o=o, kk=kk, c0=c0, c1=c1, N=N, bo=bo, osl=osl: e.matmul(
                                        ps[bo][:, 0:N], lhsT=ring[:, osl[kk], (m * 2 + o) * 128:(m * 2 + o + 1) * 128],
                                        rhs=X4[:, 2 * kk + o, c0:c1], start=(kk == 0 and o == 0), stop=(kk == 1 and o == 1)),
                                        r=[("ring", osl[kk])] + segkeys("X4", c0, c1, 2 * kk + o), w=[("ps", bo)])
                            x_update(bo, (c0, c1), m, g1)
                            cnt_items += 1
                            if cnt_items % 4 == 0 and nxt:
                                nxt.pop(0)()
                    while nxt:
                        nxt.pop(0)()
                    bank_map["OW"], bank_map["A4"] = saved
            ffn(i, BLK_B)

        import os
        stop = os.environ.get("KSTOP", "all")
        order = ["pro", "a0", "a1", "kv", "b2", "all"]
        lvl = order.index(stop)
        if lvl >= 1:
            layer_a(0)
        if lvl >= 2:
            layer_a(1)
        if lvl >= 3:
            kv_phase()
        if lvl >= 4:
            layer_b(2)
        if lvl >= 5:
            layer_b(3)
        yT_dv = yT_d.rearrange("(c p) t -> p c t", p=128)
        for bi, (c0, c1) in enumerate(BLK_B):
            d0 = 0 if c0 == 0 else 16 + (c0 - C_OWN)
            S.add("sp", lambda e, c0=c0, c1=c1, d0=d0: e.dma_start(
                out=yT_dv[:, :, d0:d0 + (c1 - c0)], in_=xT[:, :, c0:c1]),
                r=[k for kc in range(KC) for k in segkeys("x", c0, c1, kc)], dma="o_y%d" % bi)
        assert lvl < 5 or slot_i[0] == N_SLOTS, (slot_i[0], N_SLOTS)
        print('ops', len(S.ops), 'slots', slot_i[0])

        semkeys = S.analyze()
        sems = {sk: es.enter_context(nc.semaphore("s%d" % n)) for n, sk in enumerate(semkeys)}
        block = es.enter_context(nc.Block())

        def emit_engine(engname):
            def body(e):
                waited = {}
                for k, (eng, fn, r, w, dma) in enumerate(S.ops):
                    if eng != engname:
                        continue
                    for d in S.need_all[k]:
                        sk, val = S.event[d]
                        if waited.get(sk, 0) < val:
                            e.wait_ge(sems[sk], val)
                            waited[sk] = val
                    inst = fn(e)
                    if k in S.event:
                        sk, val = S.event[k]
                        inst.then_inc(sems[sk], 16 if dma is not None else 1)
                if engname == "sp":
                    for name, val in S.final_dma.items():
                        if name.startswith("o_"):
                            e.wait_ge(sems[("dma", name)], val)
            return body

        block.tensor(emit_engine("pe"))
        block.scalar(emit_engine("act"))
        block.vector(emit_engine("dve"))
        block.gpsimd(emit_engine("pool"))
        block.sync(emit_engine("sp"))
    return nc


SLOT_DESCS = []
N_SLOTS = 257


def build_wstream(inp):
    assert len(SLOT_DESCS) == N_SLOTS, len(SLOT_DESCS)
    slots = np.zeros((N_SLOTS, 128, 2048), np.float32)
    w_ada = [np.asarray(inp["w_ada"][i]).reshape(8, 128, 48, 128) for i in range(4)]
    w_ada.append(np.asarray(inp["w_ada_kv"]).reshape(8, 128, 16, 128))
    Win = [np.asarray(inp["w_ffn_in"][i]).reshape(8, 128, 2, NJ, 128) for i in range(4)]
    Wout = [np.asarray(inp["w_ffn_out"][i]).reshape(NJ, 128, 1024) for i in range(4)]
    wkv = np.asarray(inp["w_kv"])
    Wk = wkv[:, :256].reshape(8, 128, 4, 64)
    Wkd = np.concatenate([Wk, Wk], axis=3)
    for k, d in enumerate(SLOT_DESCS):
        kind = d[0]
        if kind == "ada":
            _, li, obp = d
            a = w_ada[li][:, :, 2 * obp:2 * obp + 2, :].transpose(1, 2, 0, 3)
        elif kind == "pool":
            wp = np.asarray(inp["w_pool"][d[1]]).reshape(4, 2, 128, 2, 128)
            a = wp.transpose(2, 0, 3, 1, 4)
        elif kind == "ffn_in":
            _, i, j = d
            a = Win[i][:, :, :, j, :].transpose(1, 2, 0, 3)
        elif kind == "ffn_out":
            _, i, jp = d
            a = Wout[i][jp:jp + 2].transpose(1, 0, 2)
        elif kind == "wk":
            h = d[1]
            a = Wkd[:, :, 2 * h:2 * h + 2].transpose(1, 2, 0, 3)
        elif kind == "wv":
            a = wkv[:, 256:].reshape(8, 128, 256).transpose(1, 0, 2)
        elif kind == "wq":
            _, jl, kvh = d
            Wq = np.asarray(inp["w_q"][jl]).reshape(8, 128, 8, 128)
            a = Wq[:, :, 2 * kvh:2 * kvh + 2, :].transpose(1, 2, 0, 3)
        elif kind == "wo":
            _, jl, kvh = d
            Wo = np.asarray(inp["w_o"][jl]).reshape(8, 128, 8, 128)
            a = Wo[2 * kvh:2 * kvh + 2].transpose(1, 2, 0, 3)
        else:
            raise ValueError(d)
        a = a.reshape(128, -1)
        slots[k, :, :a.shape[1]] = a
    return slots


def fm(v):
    v = np.asarray(v, np.float32)
    lead = v.shape[:-1]
    a = v.reshape(lead + (8, 128))
    a = np.moveaxis(a, -1, 0)
    return a.reshape(128, -1)


_CACHE = {}


def prep_inputs(inp):
    if "nc" not in _CACHE:
        _CACHE["nc"] = build_program()
    ws = build_wstream(inp)
    xp, xs = inp["x_prompt"], inp["x_sample"]
    in_maps = []
    for r in range(NCORES):
        pb, s, sb = r // 4, r % 4, r
        xT = np.zeros((D, T), np.float32)
        xT[:, 0:16] = xs[sb].T
        if s > 0:
            xT[:, 16:176] = xp[pb, s * NOWN - 160:s * NOWN].T
        xT[:, 176:] = xp[pb, s * NOWN:(s + 1) * NOWN].T
        cT = np.stack([fm(inp["c_prompt"][pb]), fm(inp["c_sample"][sb])], axis=2).reshape(128, 16)
        vecs = np.zeros((128, NVEC), np.float32)
        for i in range(4):
            vecs[:, V_BADA + i * 48:V_BADA + (i + 1) * 48] = inp["b_ada"][i].reshape(48, 128).T
            vecs[:, V_GMIX + i * 8:V_GMIX + (i + 1) * 8] = fm(inp["g_mix"][i])
            vecs[:, V_GFFN + i * 8:V_GFFN + (i + 1) * 8] = fm(inp["g_ffn"][i])
        for i in range(2):
            vecs[:, V_PSC + i * 8:V_PSC + (i + 1) * 8] = fm(inp["pool_scale"][i])
            vecs[:, V_GQ + i] = np.tile(inp["g_q"][i], 2)
            vecs[:, V_SINK + i * 16:V_SINK + (i + 1) * 16] = inp["sinks"][i][None, :]
        vecs[:, V_GKV:V_GKV + 8] = fm(inp["g_kv"])
        vecs[:, V_BKV:V_BKV + 16] = inp["b_ada_kv"].reshape(16, 128).T
        vecs[:, V_GK] = np.tile(inp["g_k"], 2)
        vecs[:, V_FLAG] = 1.0 if s > 0 else 0.0
        pos = s * NOWN + np.arange(16)
        for kc in range(8):
            w = 2 ** (kc // 2 + 1)
            vecs[:, V_INVC + kc * 16:V_INVC + (kc + 1) * 16] = (1.0 / np.minimum(w, pos + 1))[None, :]
        vecs[:, V_EPS] = EPS
        spT = np.zeros((128, 2, 8, 16), np.float32)
        sp = inp["state_pool"][:, sb]
        spT[:, :, :, 1:] = sp.reshape(2, 15, 8, 128).transpose(3, 0, 2, 1)
        ck = inp["cache_k"][sb]
        ckT = np.tile(ck.transpose(2, 1, 0), (2, 1, 1)).reshape(128, 512)
        cvr = inp["cache_v"][sb].reshape(128, 256)
        cv = np.ones((128, 384), np.float32)
        cv[:, 0:64] = cvr[:, 0:64]
        cv[:, 128:192] = cvr[:, 64:128]
        cv[:, 192:256] = cvr[:, 128:192]
        cv[:, 320:384] = cvr[:, 192:256]
        in_maps.append({"xT": xT, "cT": np.ascontiguousarray(cT), "vecs": vecs,
                        "spT": spT.reshape(128, 256), "ckT": np.ascontiguousarray(ckT), "cv": cv,
                        "cvraw": np.ascontiguousarray(cvr), "wstream": ws})
    return in_maps


def assemble(R, inp):
    xp = inp["x_prompt"]
    B, L = xp.shape[0], xp.shape[1]
    y_p = np.zeros((B, L, D), np.float32)
    y_s = np.zeros((NCORES, 16, D), np.float32)
    pool_p = np.zeros((2, B, 15, D), np.float32)
    k_p = np.zeros((B, 128, 4, 64), np.float32)
    v_p = np.zeros((B, 128, 4, 64), np.float32)
    pool_s = np.zeros((2, NCORES, 15, D), np.float32)
    k_s = np.zeros((NCORES, 128, 4, 64), np.float32)
    v_s = np.zeros((NCORES, 128, 4, 64), np.float32)
    for r in range(NCORES):
        pb, s, sb = r // 4, r % 4, r
        o = R[r]
        yT = np.asarray(o["yT"])
        y_s[sb] = yT[:, 0:16].T
        y_p[pb, s * NOWN:(s + 1) * NOWN] = yT[:, 16:].T
        pT = np.asarray(o["poolT"])
        for i in range(2):
            pool_s[i, sb] = pT[i][:, 1:16].T
            if s == 3:
                pool_p[i, pb] = pT[i][:, 17:32].T
        k_s[sb] = np.asarray(o["ksT"]).transpose(2, 0, 1)
        v_s[sb] = np.asarray(o["vs"]).reshape(128, 4, 64)
        if s == 3:
            k_p[pb] = np.asarray(o["knT"])[:, :, 16:144].transpose(2, 0, 1)
            v_p[pb] = np.asarray(o["vn"])[16:144].reshape(128, 4, 64)
    return (y_p, y_s, pool_p, k_p, v_p, pool_s, k_s, v_s)


def kernel(**inputs):
    inp = {k: np.asarray(v) for k, v in inputs.items()}
    if "nc" not in _CACHE:
        _CACHE["nc"] = build_program()
    nc = _CACHE["nc"]
    in_maps = prep_inputs(inp)
    res = run_bass_kernel_spmd(nc, in_maps, core_ids=list(range(NCORES)))
    return assemble(res.results, inp)
```
